# Optimizing a Trainium2 kernel written in Bass

```python
import jax, jax.numpy as jnp
from jax import lax
import numpy as np

D_MODEL = 1024
BATCH = 16
SEQ = 256
DEPTH = 1
DEC_BATCH = 2
DEC_SEQ = 1024
PAST_LEN = 256

GRID_W = 64
D_CONV = D_MODEL
CONV_W = 3
DN_HEADS = 8
DN_HEAD_DIM = 128
D_DN = DN_HEADS * DN_HEAD_DIM
DN_CONV_W = 3
CHUNK = 64
D_FF = 2816
N_ADA = 9
N_IN = 3 * D_CONV + 4 * D_DN + 2 * D_MODEL + 4 * DN_HEADS
EPS = 1e-6
POS_BASE = 10000.0

kernel_name = 'hybrid_conv_deltanet_flow_step'


def _rmsnorm(x, w):
    xf = x.astype(jnp.float32)
    y = xf * lax.rsqrt(jnp.mean(xf * xf, axis=-1, keepdims=True) + EPS)
    return (y * w.astype(jnp.float32)).astype(x.dtype)


def _modulate(x, shift, scale):
    return x * (1 + scale) + shift


def _swiglu(x, wg, wu, wd):
    return (jax.nn.silu(x @ wg) * (x @ wu)) @ wd


def _l2norm(x):
    return x * lax.rsqrt(jnp.sum(x * x, axis=-1, keepdims=True) + EPS)


def _conv_rows(x, w, n_rows):
    b, t, ch = x.shape
    width = w.shape[0]
    half = width // 2
    row_len = t // n_rows
    xp = jnp.pad(x.reshape(b, n_rows, row_len, ch), ((0, 0), (0, 0), (half, half), (0, 0)))
    y = xp[:, :, 0:row_len] * w[0]
    for i in range(1, width):
        y = y + xp[:, :, i:i + row_len] * w[i]
    return y.reshape(b, t, ch)


def _grid_pos_embed(n_rows):
    t = jnp.arange(n_rows * GRID_W)
    r = (t // GRID_W).astype(jnp.float32)
    col = (t % GRID_W).astype(jnp.float32)
    quarter = D_MODEL // 4
    omega = 1.0 / (POS_BASE ** (jnp.arange(quarter, dtype=jnp.float32) / quarter))
    ar = r[:, None] * omega
    ac = col[:, None] * omega
    return jnp.concatenate([jnp.sin(ar), jnp.cos(ar), jnp.sin(ac), jnp.cos(ac)], axis=-1)


def _gated_delta_chunked(q, k, v, g, beta, s0):
    b, h, t, dk = q.shape
    dv = v.shape[-1]
    n = t // CHUNK
    q = q.reshape(b, h, n, CHUNK, dk)
    k = k.reshape(b, h, n, CHUNK, dk)
    v = v.reshape(b, h, n, CHUNK, dv)
    g = g.reshape(b, h, n, CHUNK)
    beta = beta.reshape(b, h, n, CHUNK)
    gc = jnp.cumsum(g, axis=-1)
    incl = jnp.tril(jnp.ones((CHUNK, CHUNK), dtype=bool))
    strict = jnp.tril(jnp.ones((CHUNK, CHUNK), dtype=bool), -1)
    decay = jnp.exp(jnp.where(incl, gc[..., :, None] - gc[..., None, :], -jnp.inf))
    kb = k * beta[..., None]
    lower = jnp.where(strict, jnp.einsum('bhncd,bhnsd->bhncs', kb, k) * decay, 0.0)
    tmat = lower + jnp.eye(CHUNK, dtype=jnp.float32)
    rhs = jnp.concatenate([v * beta[..., None], kb * jnp.exp(gc)[..., None]], axis=-1)
    sol = lax.linalg.triangular_solve(tmat, rhs, left_side=True, lower=True, unit_diagonal=True)
    u = sol[..., :dv]
    w = sol[..., dv:]
    a_intra = jnp.where(incl, jnp.einsum('bhncd,bhnsd->bhncs', q, k) * decay, 0.0)
    q_dec = q * jnp.exp(gc)[..., None]
    k_dec = k * jnp.exp(gc[..., -1:] - gc)[..., None]
    g_last = jnp.exp(gc[..., -1])

    def step(s, inp):
        u_i, w_i, qd_i, kd_i, a_i, gl_i = inp
        v_new = u_i - jnp.einsum('bhck,bhkv->bhcv', w_i, s)
        o_i = jnp.einsum('bhck,bhkv->bhcv', qd_i, s) + jnp.einsum('bhcs,bhsv->bhcv', a_i, v_new)
        s = s * gl_i[..., None, None] + jnp.einsum('bhck,bhcv->bhkv', kd_i, v_new)
        return s, o_i

    xs = (jnp.moveaxis(u, 2, 0), jnp.moveaxis(w, 2, 0), jnp.moveaxis(q_dec, 2, 0),
          jnp.moveaxis(k_dec, 2, 0), jnp.moveaxis(a_intra, 2, 0), jnp.moveaxis(g_last, 2, 0))
    s_final, o = lax.scan(step, s0, xs)
    o = jnp.moveaxis(o, 0, 2).reshape(b, h, t, dv)
    return o, s_final


def _mixer(u, s0_f, s0_b, n_rows, p):
    b, t, _ = u.shape
    proj = u @ p['w_in']
    bg = proj[..., 0:D_CONV]
    cg = proj[..., D_CONV:2 * D_CONV]
    xa = proj[..., 2 * D_CONV:3 * D_CONV]
    o1 = 3 * D_CONV
    qkv = proj[..., o1:o1 + 3 * D_DN]
    z = proj[..., o1 + 3 * D_DN:o1 + 4 * D_DN]
    o2 = o1 + 4 * D_DN
    ga = proj[..., o2:o2 + D_MODEL]
    gb = proj[..., o2 + D_MODEL:o2 + 2 * D_MODEL]
    o3 = o2 + 2 * D_MODEL
    b_raw = proj[..., o3:o3 + 2 * DN_HEADS].reshape(b, t, 2, DN_HEADS).astype(jnp.float32)
    a_raw = proj[..., o3 + 2 * DN_HEADS:o3 + 4 * DN_HEADS].reshape(b, t, 2, DN_HEADS).astype(jnp.float32)

    y_a = (bg * _conv_rows(cg * xa, p['conv_w'], n_rows)) @ p['conv_out_w']

    qkv = jax.nn.silu(_conv_rows(qkv, p['dn_conv_w'], n_rows)).astype(jnp.float32)
    qkv = qkv.reshape(b, t, 3, DN_HEADS, DN_HEAD_DIM).transpose(2, 0, 3, 1, 4)
    q = _l2norm(qkv[0]) * (DN_HEAD_DIM ** -0.5)
    k = _l2norm(qkv[1])
    v = qkv[2]
    beta = jax.nn.sigmoid(b_raw).transpose(2, 0, 3, 1)
    g = (-jnp.exp(p['dn_a_log'].astype(jnp.float32))
         * jax.nn.softplus(a_raw + p['dn_dt_bias'].astype(jnp.float32))).transpose(2, 0, 3, 1)
    o_f, s_f = _gated_delta_chunked(q, k, v, g[0], beta[0], s0_f.astype(jnp.float32))
    o_b, s_b = _gated_delta_chunked(jnp.flip(q, 2), jnp.flip(k, 2), jnp.flip(v, 2),
                                    jnp.flip(g[1], -1), jnp.flip(beta[1], -1), s0_b.astype(jnp.float32))
    o = (o_f + jnp.flip(o_b, 2)).transpose(0, 2, 1, 3)
    o = (o * lax.rsqrt(jnp.mean(o * o, axis=-1, keepdims=True) + EPS)
         * p['dn_norm_w'].astype(jnp.float32)
         * jax.nn.silu(z.astype(jnp.float32).reshape(b, t, DN_HEADS, DN_HEAD_DIM)))
    y_b = o.reshape(b, t, D_DN).astype(u.dtype) @ p['dn_out_w']

    mix = (jax.nn.sigmoid(ga) * y_a + jax.nn.sigmoid(gb) * y_b) @ p['w_o']
    return mix, s_f, s_b


def _layer(x, cond, s0_f, s0_b, n_rows, p):
    mod = (jax.nn.silu(cond) @ p['ada_w'] + p['ada_b']).reshape(cond.shape[0], 1, N_ADA, D_MODEL)
    sh1, sc1, g1 = mod[:, :, 0], mod[:, :, 1], mod[:, :, 2]
    sh2, sc2, g2 = mod[:, :, 3], mod[:, :, 4], mod[:, :, 5]
    sh3, sc3, g3 = mod[:, :, 6], mod[:, :, 7], mod[:, :, 8]
    x = x + 0.5 * g1 * _swiglu(_modulate(_rmsnorm(x, p['norm_ffn1']), sh1, sc1),
                               p['ffn1_w_gate'], p['ffn1_w_up'], p['ffn1_w_down'])
    mix, s_f, s_b = _mixer(_modulate(_rmsnorm(x, p['norm_mix']), sh2, sc2), s0_f, s0_b, n_rows, p)
    x = x + g2 * mix
    x = x + 0.5 * g3 * _swiglu(_modulate(_rmsnorm(x, p['norm_ffn2']), sh3, sc3),
                               p['ffn2_w_gate'], p['ffn2_w_up'], p['ffn2_w_down'])
    return x, s_f, s_b


def setup_inputs(seed: int = 0) -> dict:
    key = jax.random.key(seed)
    ks = jax.random.split(key, 32)
    f32 = jnp.float32
    L = DEPTH

    def nrm(k, shape, scale):
        return jax.random.normal(k, shape, f32) * scale

    dt = jnp.exp(jax.random.uniform(ks[18], (L, 2, DN_HEADS), f32, np.log(1e-3), np.log(1e-1)))
    return {
        'x_prompt': nrm(ks[0], (BATCH, SEQ, D_MODEL), 1.0),
        'x_sample': nrm(ks[1], (DEC_BATCH, DEC_SEQ, D_MODEL), 1.0),
        'state_dn_fwd': nrm(ks[2], (DEC_BATCH, L, DN_HEADS, DN_HEAD_DIM, DN_HEAD_DIM), 0.05),
        'state_dn_bwd': nrm(ks[3], (DEC_BATCH, L, DN_HEADS, DN_HEAD_DIM, DN_HEAD_DIM), 0.05),
        'c': nrm(ks[4], (DEC_BATCH, D_MODEL), 1.0),
        'c_ctx': nrm(ks[5], (D_MODEL,), 1.0),
        'ada_w': nrm(ks[6], (L, D_MODEL, N_ADA * D_MODEL), 0.5 * D_MODEL ** -0.5),
        'ada_b': nrm(ks[7], (L, N_ADA * D_MODEL), 0.02),
        'norm_ffn1': 1.0 + nrm(ks[8], (L, D_MODEL), 0.02),
        'ffn1_w_gate': nrm(ks[9], (L, D_MODEL, D_FF), D_MODEL ** -0.5),
        'ffn1_w_up': nrm(ks[10], (L, D_MODEL, D_FF), D_MODEL ** -0.5),
        'ffn1_w_down': nrm(ks[11], (L, D_FF, D_MODEL), D_FF ** -0.5),
        'norm_mix': 1.0 + nrm(ks[12], (L, D_MODEL), 0.02),
        'w_in': nrm(ks[13], (L, D_MODEL, N_IN), D_MODEL ** -0.5),
        'conv_w': nrm(ks[14], (L, CONV_W, D_CONV), CONV_W ** -0.5),
        'conv_out_w': nrm(ks[15], (L, D_CONV, D_MODEL), D_CONV ** -0.5),
        'dn_conv_w': nrm(ks[16], (L, DN_CONV_W, 3 * D_DN), DN_CONV_W ** -0.5),
        'dn_a_log': jnp.log(jax.random.uniform(ks[17], (L, 2, DN_HEADS), f32, 1.0, 16.0)),
        'dn_dt_bias': dt + jnp.log(-jnp.expm1(-dt)),
        'dn_norm_w': 1.0 + nrm(ks[19], (L, DN_HEAD_DIM), 0.02),
        'dn_out_w': nrm(ks[20], (L, D_DN, D_MODEL), D_DN ** -0.5),
        'w_o': nrm(ks[21], (L, D_MODEL, D_MODEL), D_MODEL ** -0.5),
        'norm_ffn2': 1.0 + nrm(ks[22], (L, D_MODEL), 0.02),
        'ffn2_w_gate': nrm(ks[23], (L, D_MODEL, D_FF), D_MODEL ** -0.5),
        'ffn2_w_up': nrm(ks[24], (L, D_MODEL, D_FF), D_MODEL ** -0.5),
        'ffn2_w_down': nrm(ks[25], (L, D_FF, D_MODEL), D_FF ** -0.5),
        'norm_f': 1.0 + nrm(ks[26], (D_MODEL,), 0.02),
    }


def reference(x_prompt, x_sample, state_dn_fwd, state_dn_bwd, c, c_ctx, ada_w, ada_b,
              norm_ffn1, ffn1_w_gate, ffn1_w_up, ffn1_w_down, norm_mix, w_in, conv_w,
              conv_out_w, dn_conv_w, dn_a_log, dn_dt_bias, dn_norm_w, dn_out_w, w_o,
              norm_ffn2, ffn2_w_gate, ffn2_w_up, ffn2_w_down, norm_f):
    def layer_params(l):
        return {'ada_w': ada_w[l], 'ada_b': ada_b[l], 'norm_ffn1': norm_ffn1[l],
                'ffn1_w_gate': ffn1_w_gate[l], 'ffn1_w_up': ffn1_w_up[l], 'ffn1_w_down': ffn1_w_down[l],
                'norm_mix': norm_mix[l], 'w_in': w_in[l], 'conv_w': conv_w[l], 'conv_out_w': conv_out_w[l],
                'dn_conv_w': dn_conv_w[l], 'dn_a_log': dn_a_log[l], 'dn_dt_bias': dn_dt_bias[l],
                'dn_norm_w': dn_norm_w[l], 'dn_out_w': dn_out_w[l], 'w_o': w_o[l],
                'norm_ffn2': norm_ffn2[l], 'ffn2_w_gate': ffn2_w_gate[l], 'ffn2_w_up': ffn2_w_up[l],
                'ffn2_w_down': ffn2_w_down[l]}

    h = x_prompt
    zero_state = jnp.zeros((x_prompt.shape[0], DN_HEADS, DN_HEAD_DIM, DN_HEAD_DIM), jnp.float32)
    states_f = []
    states_b = []
    for l in range(DEPTH):
        h, s_f, s_b = _layer(h, c_ctx[None, :], zero_state, zero_state, 1, layer_params(l))
        states_f.append(s_f)
        states_b.append(s_b)
    y_prompt = _rmsnorm(h, norm_f)
    new_state_dn_fwd = jnp.stack(states_f, axis=1).astype(x_prompt.dtype)
    new_state_dn_bwd = jnp.stack(states_b, axis=1).astype(x_prompt.dtype)

    rows = x_sample.shape[1] // GRID_W
    zt = x_sample + _grid_pos_embed(rows).astype(x_sample.dtype)[None]
    for l in range(DEPTH):
        zt, _, _ = _layer(zt, c, state_dn_fwd[:, l], state_dn_bwd[:, l], rows, layer_params(l))
    y_sample = _rmsnorm(zt, norm_f)
    return (y_prompt, y_sample, new_state_dn_fwd, new_state_dn_bwd)
```

```python
import os
import math
import numpy as np
from contextlib import ExitStack
import concourse.bass as bass
import concourse.mybir as mybir
from concourse.bass_utils import run_bass_kernel_spmd

F32 = mybir.dt.float32
BF16 = mybir.dt.bfloat16
I32 = mybir.dt.int32
AF = mybir.ActivationFunctionType
ALU = mybir.AluOpType

T = 1024
D = 1024
NT = 8
FF = 2816
NFT = 22
H = 8
NIN = 9248
EPS = 1e-6
STAGE = int(os.environ.get("MK_STAGE", "9"))
SUB = int(os.environ.get("MK_SUB", "9"))
CUT = int(os.environ.get("MK_CUT", "9"))
DBG = int(os.environ.get("MK_DBG", "0"))
_DBG = {}


class Sched:
    ENG = ("pe", "act", "dve", "pool", "sp")

    def __init__(self, nc, es):
        self.nc = nc
        self.es = es
        self.prog = {e: [] for e in self.ENG}
        self.sems = {}
        self.vals = {}
        self.seen = {e: {} for e in self.ENG}
        self.lastw = {}
        self.readers = {}

    def sem(self, key):
        if key not in self.sems:
            self.sems[key] = self.es.enter_context(self.nc.semaphore("s_" + str(key).replace(" ", "")[:40]))
        return self.sems[key]

    def op(self, eng, fn, reads=(), writes=(), semkey=None, n=1, dma=False):
        psr = [k for k in reads if isinstance(k, tuple) and k and k[0] == "ps"]
        if psr:
            reads = [k for k in reads if k not in psr]
            writes = list(writes) + psr
        need = {}
        for k in reads:
            if k in self.lastw:
                sk, v = self.lastw[k]
                need[sk] = max(need.get(sk, 0), v)
        for k in writes:
            if k in self.lastw:
                sk, v = self.lastw[k]
                need[sk] = max(need.get(sk, 0), v)
            for sk, v in self.readers.get(k, ()):
                need[sk] = max(need.get(sk, 0), v)
        waits = []
        for sk, v in need.items():
            if eng == "pe" and sk == "pe":
                continue
            if self.seen[eng].get(sk, 0) >= v:
                continue
            self.seen[eng][sk] = v
            waits.append((sk, v))
        sk = semkey or eng
        unit = 16 if dma else 1
        self.vals[sk] = self.vals.get(sk, 0) + unit * n
        me = (sk, self.vals[sk])
        self.sem(sk)
        self.prog[eng].append((waits, fn, sk, unit))
        for k in reads:
            self.readers.setdefault(k, []).append(me)
        for k in writes:
            self.lastw[k] = me
            self.readers[k] = []
        return me

    def emit(self, eng, e):
        for waits, fn, sk, unit in self.prog[eng]:
            for wk, v in waits:
                e.wait_ge(self.sems[wk], v)
            insts = fn(e)
            if insts is None:
                continue
            if not isinstance(insts, (list, tuple)):
                insts = [insts]
            for i in insts:
                i.then_inc(self.sems[sk], unit)


def build_nc():
    nc = bass.Bass("TRN2", target_bir_lowering=False)

    def din(name, shape):
        return nc.dram_tensor(name, list(shape), F32, kind="ExternalInput").ap()

    def dout(name, shape):
        return nc.dram_tensor(name, list(shape), F32, kind="ExternalOutput").ap()

    x_in = din("x", [T, D])
    s0f_in = din("s0f", [H, 128, 128])
    s0b_in = din("s0b", [H, 128, 128])
    pab_in = din("pab", [256, 128])
    cl_in = din("cl", [128, 16])
    cf_in = din("cf", [128, 192])
    cb_in = din("cb", [128, 9 * 128])
    ada_w = din("ada_w", [D, 9 * D])
    w_g = [din("ffn1_w_gate", [D, FF]), din("ffn2_w_gate", [D, FF])]
    w_u = [din("ffn1_w_up", [D, FF]), din("ffn2_w_up", [D, FF])]
    w_d = [din("ffn1_w_down", [FF, D]), din("ffn2_w_down", [FF, D])]
    w_in = din("w_in", [D, NIN])
    conv_out_w = din("conv_out_w", [D, D])
    dn_out_w = din("dn_out_w", [D, D])
    w_o = din("w_o", [D, D])
    y_out = dout("y", [T, D])
    sf_out = dout("sf", [4, H, 128, 128])
    sb_out = dout("sb", [4, H, 128, 128])
    if DBG:
        dbg32 = dout("dbg32", [128, 8192])
        dbgb = nc.dram_tensor("dbgb", [128, 16384], BF16, kind="ExternalOutput").ap()
    dbg_off = {"f": 0, "b": 0}
    dbg_map = {}

    with ExitStack() as es:
        S = Sched(nc, es)

        def sb(name, shape, dt):
            return es.enter_context(nc.sbuf_tensor("t_" + name, list(shape), dt))

        xT = sb("xT", [128, NT, T], F32)
        uT = sb("uT", [128, NT, T], BF16)
        arena = sb("arena", [128, 47872], BF16)
        RING = 4
        LOOK = 1
        ring = [sb(f"ring{i}", [128, 4096], BF16) for i in range(RING)]
        cf = sb("cf", [128, 192], F32)
        cb = sb("cb", [128, 1152], BF16)
        cl = sb("cl", [128, 16], F32)
        pab = sb("pab", [128, 2, 128], F32)
        PT = sb("PT", [128, 256], F32)
        mod = sb("mod", [128, 72], F32)
        modx = sb("modx", [128, 64], F32)
        scb = sb("scb", [128, NT], BF16)
        sc32 = sb("sc32", [128, NT], F32)
        rstd = sb("rstd", [128, T], F32)
        scr = [sb(f"scr{i}", [128, T], F32) for i in range(2)]
        scrb = [sb(f"scrb{i}", [128, T], BF16) for i in range(2)]

        mx = sb("mx", [128, 64], F32)
        colsT = sb("colsT", [128, 5, 128], F32)
        sel = sb("sel", [16, 2, 128], BF16)
        gchl = sb("gchl", [16, 2, 1024], BF16)
        glc = sb("glc", [128, 4, 16], F32)
        gcb = sb("gcb", [128, T], F32)
        onec = sb("onec", [128, 1], F32)
        so_rr = [0]
        outkeys = []
        ident32 = cf[:, 64:192]
        identb = cb[:, 0:128]
        onesb = cb[:, 128:256]

        pst = [es.enter_context(nc.psum_tensor(f"ps{i}", [128, 1024], F32)) for i in range(4)]
        ps_rr = [0, 0, 0]

        ps_mode = ["all"]

        def ps_pair():
            if ps_mode[0] == "dn":
                i = ps_rr[2] % 3
                ps_rr[2] += 1
            else:
                i = ps_rr[0] % 4
                ps_rr[0] += 1
            return pst[i], [("ps", 2 * i), ("ps", 2 * i + 1)]

        wq = []
        wq_issued = [0]

        DN_LO, DN_HI = 37, 45
        ring_extra = {}

        def look_of(i):
            if STAGE < 2:
                return 1
            if i < DN_LO:
                return 2 if i < DN_LO - 1 else 1
            if i <= DN_HI:
                return 1
            if 50 <= i <= 57:
                return 2
            return 3

        def slot_of(i):
            if STAGE >= 2 and DN_LO <= i <= DN_HI:
                return i % 2
            if STAGE >= 2 and i > DN_HI:
                return (i - DN_HI - 1) % RING
            return i % RING

        def wq_issue_upto(k):
            while wq_issued[0] < min(k, len(wq)):
                i = wq_issued[0]
                slot = slot_of(i)
                pairs = wq[i](ring[slot])

                def fn(e, pairs=pairs):
                    return [e.dma_start(out=d, in_=s) for d, s in pairs]
                S.op("pool", fn, writes=[("ring", slot)] + list(ring_extra.get(slot, ())), semkey=("ringsem", slot), n=len(pairs), dma=True)
                wq_issued[0] += 1

        wq_next = [0]

        def wget():
            i = wq_next[0]
            wq_next[0] += 1
            wq_issue_upto(i + 1 + look_of(i))
            return ring[slot_of(i)], ("ring", slot_of(i))

        def chunk_cols(wap, c0, ncols):
            def mk(r, c0=c0, ncols=ncols):
                v = r[:, 0:8 * ncols].rearrange("p (c n) -> p c n", c=8)
                src = wap.rearrange("(c p) n -> p c n", p=128)
                return [(v[:, c4:c4 + 4, :], src[:, c4:c4 + 4, c0:c0 + ncols]) for c4 in (0, 4)]
            return mk

        def chunk_gu(l, ft2):
            def mk(r):
                v = r[:, 0:4096].rearrange("p (m c n) -> p m c n", m=2, c=8)
                out = []
                for m, wap in ((0, w_g[l]), (1, w_u[l])):
                    src = wap.rearrange("(c p) n -> p c n", p=128)
                    for c in range(8):
                        out.append((v[:, m, c, :], src[:, c, ft2 * 256:(ft2 + 1) * 256]))
                return out
            return mk

        def chunk_down(l, j):
            def mk(r):
                v = r[:, 0:NFT * 128].rearrange("p (f n) -> p f n", f=NFT)
                src = w_d[l].rearrange("(f p) n -> p f n", p=128)
                return [(v[:, f0:f0 + 6, :], src[:, f0:f0 + 6, j * 128:(j + 1) * 128]) for f0 in (0, 6, 12)] + \
                       [(v[:, 18:22, :], src[:, 18:22, j * 128:(j + 1) * 128])]
            return mk

        def chunk_head(h):
            def mk(r):
                v = r[:, 0:4096].rearrange("p (m c n) -> p m c n", m=4, c=8)
                src = w_in.rearrange("(c p) n -> p c n", p=128)
                out = []
                for m in range(4):
                    c0 = 3072 + m * 1024 + h * 128
                    for c0_, c1_ in ((0, 4), (4, 8)):
                        out.append((v[:, m, c0_:c1_, :], src[:, c0_:c1_, c0:c0 + 128]))
                return out
            return mk

        def chunk_ab():
            def mk(r):
                v = r[:, 0:256].rearrange("p (c n) -> p c n", c=8)
                src = w_in.rearrange("(c p) n -> p c n", p=128)
                return [(v, src[:, :, 9216:9248])]
            return mk

        for g in range(6):
            wq.append(chunk_cols(ada_w[:, :], g * 512, 512))
        for ft2 in range(11):
            wq.append(chunk_gu(0, ft2))
            wq.append(chunk_cols(ada_w[:, :], (6 + ft2) * 512, 512))
        wq.append(chunk_down(0, 0))
        wq.append(chunk_cols(ada_w[:, :], 17 * 512, 512))
        for j in range(1, NT):
            wq.append(chunk_down(0, j))
        if STAGE >= 2:
            wq.append(chunk_ab())
            for h in range(H):
                wq.append(chunk_head(h))
            def mk_pair(wa, wb, cb0, g):
                def mk(r):
                    v = r[:, 0:4096].rearrange("p (m c n) -> p m c n", m=2, c=8)
                    sa = wa.rearrange("(c p) n -> p c n", p=128)
                    sb_ = wb.rearrange("(c p) n -> p c n", p=128)
                    return [(v[:, 0], sa[:, :, g * 256:(g + 1) * 256]), (v[:, 1], sb_[:, :, cb0 + g * 256:cb0 + (g + 1) * 256])]
                return mk
            for g in range(4):
                wq.append(mk_pair(dn_out_w, w_in, 8192, g))
            for j2 in range(4):
                def mk_cx(r, j2=j2):
                    v = r[:, 0:4096].rearrange("p (m c n) -> p m c n", m=2, c=8)
                    src = w_in.rearrange("(c p) n -> p c n", p=128)
                    return [(v[:, m_], src[:, :, (1 + m_) * 1024 + j2 * 256:(1 + m_) * 1024 + (j2 + 1) * 256]) for m_ in range(2)]
                wq.append(mk_cx)
                wq.append(chunk_cols(w_in, j2 * 256, 256))
            for g in range(4):
                wq.append(mk_pair(conv_out_w, w_in, 7168, g))
            for g in range(2):
                wq.append(chunk_cols(w_o, g * 512, 512))
        for ft2 in range(11):
            wq.append(chunk_gu(1, ft2))
        for j in range(NT):
            wq.append(chunk_down(1, j))

        def dve(fn, reads, writes):
            return S.op("dve", fn, reads, writes)

        def act(fn, reads, writes):
            return S.op("act", fn, reads, writes)

        def pe(fn, reads, writes):
            return S.op("pe", fn, reads, writes)

        def hw_dma(out, in_, reads, writes, semkey):
            return S.op("sp", lambda e: e.dma_start(out=out, in_=in_), reads, writes, semkey=semkey, dma=True)

        def dbg(name, ap, reads, n=None):
            if not DBG or name in dbg_map:
                return
            is32 = ap.dtype == F32
            kind = "f" if is32 else "b"
            npart, nfree = ap.shape[0], int(np.prod(ap.shape[1:]))
            o = dbg_off[kind]
            dbg_off[kind] += nfree
            dbg_map[name] = (kind, o, npart, tuple(ap.shape[1:]))
            dst = (dbg32 if is32 else dbgb)[0:npart, o:o + nfree]
            if len(ap.shape) == 3:
                dst = dst.rearrange("p (a n) -> p a n", a=ap.shape[1])
            hw_dma(dst, ap, reads, [("dbgout", name)], ("dbg", len(dbg_map) % 8))
            outkeys.append(("dbgout", name))

        hw_dma(cf[:], cf_in[:, :], [], ["cf"], "ld_c0")
        hw_dma(cl[:], cl_in[:, :], [], ["cl"], "ld_c1")
        hw_dma(pab[:], pab_in.rearrange("(a p) n -> p a n", p=128), [], ["pab"], "ld_c2")
        S.op("pool", lambda e: e.dma_start(out=cb[:], in_=cb_in[:, :]), [], ["cb"], semkey="ld_cb", dma=True)

        psP, kP = ps_pair()

        def f(e):
            e.transpose(psP[:, 0:128], pab[:, 0, :], ident32)
            return e.transpose(psP[:, 128:256], pab[:, 1, :], ident32)
        pe(f, ["pab", "cf"], kP)
        dve(lambda e: e.tensor_copy(out=PT[:], in_=psP[:, 0:256]), kP, ["PT"])
        act(lambda e: e.activation(out=sc32[:], in_=PT[:, 104:112], func=AF.Silu), ["PT"], ["sc32"])
        dve(lambda e: e.tensor_copy(out=scb[:], in_=sc32[:]), ["sc32"], ["scb"])

        def ada_group(g0, g1):
            psM, kM = ps_pair()
            for g in range(g0, g1):
                r, rk = wget()
                v = r[:, 0:4096].rearrange("p (c n) -> p c n", c=8)

                def f(e, v=v, g=g):
                    last = None
                    for t4 in range(4):
                        col = g * 4 + t4
                        for c in range(8):
                            last = e.matmul(psM[:, col:col + 1], lhsT=v[:, c, t4 * 128:(t4 + 1) * 128], rhs=scb[:, c:c + 1],
                                            start=(c == 0), stop=(c == 7))
                    return last
                pe(f, [rk, "scb"], kM)
            c0, c1 = g0 * 4, g1 * 4
            dve(lambda e: e.tensor_tensor(out=mod[:, c0:c1], in0=psM[:, c0:c1], in1=PT[:, c0:c1], op=ALU.add), kM + ["PT"], [("mod", g0)])

        def ada_chunk(g):
            r, rk = wget()
            v = r[:, 0:4096].rearrange("p (c n) -> p c n", c=8)
            psM, kM = ps_pair()

            def f(e, v=v, psM=psM):
                last = None
                for t4 in range(4):
                    for c in range(8):
                        last = e.matmul(psM[:, t4:t4 + 1], lhsT=v[:, c, t4 * 128:(t4 + 1) * 128], rhs=scb[:, c:c + 1], start=(c == 0), stop=(c == 7))
                return last
            pe(f, [rk, "scb"], kM[:1])
            c0 = g * 4
            dve(lambda e, psM=psM, c0=c0: e.tensor_tensor(out=mod[:, c0:c0 + 4], in0=psM[:, 0:4], in1=PT[:, c0:c0 + 4], op=ALU.add),
                kM[:1] + ["PT"], [("mod", 6 if g < 12 else 12)])

        def mod_derive(k, normcol, gscale):
            sh, sc_, gt = 3 * k, 3 * k + 1, 3 * k + 2
            dve(lambda e: e.scalar_tensor_tensor(out=modx[:, k * 8:k * 8 + 8], in0=mod[:, sc_ * 8:sc_ * 8 + 8], scalar=1.0,
                                                 in1=PT[:, normcol:normcol + 8], op0=ALU.add, op1=ALU.mult),
                [("mod", 0), ("mod", 6), ("mod", 12), "PT"], [("modx", k)])
            dve(lambda e: e.tensor_scalar(out=modx[:, 24 + k * 8:24 + k * 8 + 8], in0=mod[:, gt * 8:gt * 8 + 8], scalar1=gscale, scalar2=None,
                                          op0=ALU.mult),
                [("mod", 0), ("mod", 6), ("mod", 12)], [("modg", k)])

        xs = arena[:, 0:16384].bitcast(F32).rearrange("p (a n) -> p a n", a=8)
        psg = arena[:, 16384:26624]
        for tt in range(8):
            hw_dma(xs[:, tt, :], x_in[tt * 128:(tt + 1) * 128, :], [], [("xs", tt)], ("ld_x", tt))
        for half in range(2):
            for dj in range(NT):
                psX, kX = ps_pair()

                def f(e, dj=dj, half=half, psX=psX):
                    last = None
                    for q in range(4):
                        tt = half * 4 + q
                        last = e.transpose(psX[:, q * 128:(q + 1) * 128], xs[:, tt, dj * 128:(dj + 1) * 128], ident32)
                    return last
                pe(f, [("xs", half * 4 + q) for q in range(4)] + ["cf"], kX[:1])
                dst = xT[:, dj, half * 512:(half + 1) * 512]
                if dj % 2 == 0:
                    dve(lambda e, dst=dst, psX=psX: e.tensor_copy(out=dst, in_=psX[:, 0:512]), kX[:1], [("xT", dj, half)])
                else:
                    act(lambda e, dst=dst, psX=psX: e.activation(out=dst, in_=psX[:, 0:512], func=AF.Identity), kX[:1], [("xT", dj, half)])
        pose = arena[:, 40000:41616].bitcast(F32)
        pe_om = pose[:, 0:2]
        act(lambda e: e.activation(out=pe_om, in_=cl[:, 7:9], func=AF.Exp, scale=-math.log(10000.0) / 256.0), ["cl"], ["pe_om"])
        ANG = pose[:, 8:168].rearrange("p (j n) -> p j n", j=2)
        PS_ = pose[:, 168:328].rearrange("p (j n) -> p j n", j=2)
        PC_ = pose[:, 328:488].rearrange("p (j n) -> p j n", j=2)
        PT1 = pose[:, 488:648].rearrange("p (j n) -> p j n", j=2)
        PT2 = pose[:, 648:808].rearrange("p (j n) -> p j n", j=2)
        for jj in range(2):
            dve(lambda e, jj=jj: e.tensor_scalar(out=ANG[:, jj, 0:16], in0=cf[:, 0:16], scalar1=pose[:, jj:jj + 1], scalar2=None, op0=ALU.mult),
                ["cf", "pe_om"], ["pe_ang"])
            dve(lambda e, jj=jj: e.tensor_scalar(out=ANG[:, jj, 16:80], in0=cf[:, 0:64], scalar1=pose[:, jj:jj + 1], scalar2=None, op0=ALU.mult),
                ["cf", "pe_om"], ["pe_ang"])
        dve(lambda e: e.memset(pose[:, 4:5], math.pi / 2), [], ["pe_hpi"])
        act(lambda e: e.activation(out=PS_, in_=ANG, func=AF.Sin, scale=1.0 / 32), ["pe_ang"], ["pe_s"])
        act(lambda e: e.activation(out=PC_, in_=ANG, func=AF.Sin, scale=-1.0 / 32, bias=pose[:, 4:5]), ["pe_ang", "pe_hpi"], ["pe_c"])
        for it in range(5):
            dve(lambda e: e.tensor_tensor(out=PT1, in0=PC_, in1=PC_, op=ALU.mult), ["pe_c"], ["pe_t1"])
            dve(lambda e: e.tensor_tensor(out=PT2, in0=PS_, in1=PS_, op=ALU.mult), ["pe_s"], ["pe_t2"])
            dve(lambda e: e.scalar_tensor_tensor(out=PS_, in0=PS_, scalar=2.0, in1=PC_, op0=ALU.mult, op1=ALU.mult), ["pe_s", "pe_c", "pe_t2"], ["pe_s"])
            dve(lambda e: e.tensor_tensor(out=PC_, in0=PT1, in1=PT2, op=ALU.subtract), ["pe_t1", "pe_t2", "pe_s"], ["pe_c"])
        dve(lambda e: e.tensor_scalar(out=PS_, in0=PS_, scalar1=cl[:, 3:4], scalar2=None, op0=ALU.mult), ["pe_s", "cl"], ["pe_s"])
        dve(lambda e: e.tensor_scalar(out=PC_, in0=PC_, scalar1=cl[:, 3:4], scalar2=None, op0=ALU.mult), ["pe_c", "cl"], ["pe_c"])
        for j in range(NT):
            tab = PS_ if (j // 2) % 2 == 0 else PC_
            jj = j % 2
            if j < 4:
                src = tab[:, jj, 0:16].unsqueeze(2).to_broadcast([128, 16, 64])
            else:
                src = tab[:, jj, 16:80].unsqueeze(1).to_broadcast([128, 16, 64])
            xv = xT[:, j, :].rearrange("p (r c) -> p r c", r=16)
            dve(lambda e, xv=xv, src=src: e.tensor_tensor(out=xv, in0=xv, in1=src, op=ALU.add),
                ["pe_s", "pe_c", ("xT", j, 0), ("xT", j, 1)], [("xT", j, 0), ("xT", j, 1)])

        def xkeys(j):
            return [("xT", j, 0), ("xT", j, 1)]

        def norm(k):
            psN, kN = ps_pair()
            for j in range(NT):
                sq = scrb[j % 2]
                act(lambda e, j=j, sq=sq: e.activation(out=sq[:], in_=xT[:, j, :], func=AF.Square), xkeys(j), [("scrb", j % 2)])

                def f(e, j=j, sq=sq):
                    e.matmul(psN[:, 0:512], lhsT=onesb, rhs=sq[:, 0:512], start=(j == 0), stop=(j == NT - 1))
                    return e.matmul(psN[:, 512:1024], lhsT=onesb, rhs=sq[:, 512:1024], start=(j == 0), stop=(j == NT - 1))
                pe(f, [("scrb", j % 2), "cb"], kN)
            act(lambda e: e.activation(out=rstd[:], in_=psN[:], func=AF.Ln, scale=1.0 / D, bias=epsc[:, 0:1]), kN + ["epsc"], ["rstd"])
            act(lambda e: e.activation(out=rstd[:], in_=rstd[:], func=AF.Exp, scale=-0.5), ["rstd"], ["rstd"])

        epsc = sb("epsc", [128, 1], F32)
        dve(lambda e: e.memset(epsc[:], EPS), [], ["epsc"])
        dve(lambda e: e.memset(onec[:], 1.0), [], ["onec"])

        def norm_apply(k):
            for j in range(NT):
                tmp = scr[j % 2]
                dve(lambda e, j=j, tmp=tmp: e.scalar_tensor_tensor(out=tmp[:], in0=xT[:, j, :], scalar=modx[:, k * 8 + j:k * 8 + j + 1],
                                                                   in1=rstd[:], op0=ALU.mult, op1=ALU.mult),
                    xkeys(j) + ["rstd", ("modx", k)], [("scr", j % 2)])
                shc = (3 * k) * 8 + j
                act(lambda e, j=j, tmp=tmp, shc=shc: e.activation(out=uT[:, j, :], in_=tmp[:], func=AF.Identity, bias=mod[:, shc:shc + 1]),
                    [("scr", j % 2), ("mod", 0), ("mod", 6), ("mod", 12)], [("uT", j)])

        hT = arena[:, 0:NFT * T].rearrange("p (f t) -> p f t", f=NFT)

        def ffn(l, k, ada_ride=False):
            norm(k)
            norm_apply(k)
            ukeys = [("uT", j) for j in range(NT)]
            for ft2 in range(11):
                if ada_ride and ft2 > 0:
                    ada_chunk(5 + ft2)
                r, rk = wget()
                v = r[:, 0:4096].rearrange("p (m c n) -> p m c n", m=2, c=8)
                for fi in range(2):
                    ft = ft2 * 2 + fi
                    psG, kG = ps_pair()
                    psU, kU = ps_pair()

                    def mm_gu(e, v=v, fi=fi, ps=psG, m=0):
                        last = None
                        for half in range(2):
                            for c in range(8):
                                last = e.matmul(ps[:, half * 512:(half + 1) * 512], lhsT=v[:, m, c, fi * 128:(fi + 1) * 128],
                                                rhs=uT[:, c, half * 512:(half + 1) * 512], start=(c == 0), stop=(c == 7))
                        return last
                    pe(mm_gu, [rk] + ukeys, kG)
                    pe(lambda e, g=mm_gu, v=v, fi=fi, ps=psU: g(e, v, fi, ps, 1), [rk] + ukeys, kU)
                    sg = scrb[ft % 2]
                    act(lambda e, sg=sg, psG=psG: e.activation(out=sg[:], in_=psG[:], func=AF.Silu), kG, [("scrb", ft % 2)])
                    dve(lambda e, sg=sg, psU=psU, ft=ft: e.tensor_tensor(out=hT[:, ft, :], in0=sg[:], in1=psU[:], op=ALU.mult),
                        kU + [("scrb", ft % 2)], [("hT", ft)])
            hkeys = [("hT", ft) for ft in range(NFT)]
            for j in range(NT):
                if ada_ride and j == 0:
                    ada_chunk(16)
                if ada_ride and j == 1:
                    ada_chunk(17)
                r, rk = wget()
                v = r[:, 0:NFT * 128].rearrange("p (f n) -> p f n", f=NFT)
                psD, kD = ps_pair()

                def f(e, v=v, psD=psD):
                    last = None
                    for half in range(2):
                        for ft in range(NFT):
                            last = e.matmul(psD[:, half * 512:(half + 1) * 512], lhsT=v[:, ft, :], rhs=hT[:, ft, half * 512:(half + 1) * 512],
                                            start=(ft == 0), stop=(ft == NFT - 1))
                    return last
                pe(f, [rk] + hkeys, kD)
                gc_ = 24 + k * 8 + j
                dve(lambda e, j=j, psD=psD, gc_=gc_: e.scalar_tensor_tensor(out=xT[:, j, :], in0=psD[:], scalar=modx[:, gc_:gc_ + 1], in1=xT[:, j, :],
                                                                            op0=ALU.mult, op1=ALU.add),
                    kD + xkeys(j) + [("modg", k)], xkeys(j))

        def mixer():
            norm(1)
            norm_apply(1)
            ukeys = [("uT", j) for j in range(NT)]
            dn_keys = set()

            def K(*k):
                dn_keys.add(k)
                return k

            def ps_bank():
                i = (6 + ps_rr[1] % 2) if ps_mode[0] == "dn" else ps_rr[1] % 8
                ps_rr[1] += 1
                return pst[i // 2][:, (i % 2) * 512:(i % 2) * 512 + 512], [("ps", i)]

            def proj(ps, wv, cols, keys_w, rk):
                def f(e):
                    last = None
                    for half in range(2):
                        for c in range(8):
                            last = e.matmul(ps[:, half * 512:(half + 1) * 512], lhsT=wv[:, c, cols[0]:cols[1]],
                                            rhs=uT[:, c, half * 512:(half + 1) * 512], start=(c == 0), stop=(c == 7))
                    return last
                pe(f, [rk] + ukeys, keys_w)

            Fb = mx[:, 0:15]
            dve(lambda e: e.tensor_copy(out=Fb, in_=cl[:, 4:5].to_broadcast([128, 15])), ["cl"], ["Fb"])
            dve(lambda e: e.memset(mx[:, 3:15:4], 1.0), ["Fb"], ["Fb"])
            tf = mx[:, 16:31]

            def conv(src, srck, dst, dstk, wc):
                w0, w1, w2 = (PT[:, c:c + 1] for c in wc)
                act(lambda e: e.activation(out=dst[:], in_=src[:, 0:T], func=AF.Identity, scale=w1), srck + ["PT"], dstk)
                dve(lambda e: e.scalar_tensor_tensor(out=dst[:, 1:T], in0=src[:, 0:T - 1], scalar=w0, in1=dst[:, 1:T], op0=ALU.mult, op1=ALU.add),
                    srck + ["PT"] + dstk, dstk)
                dve(lambda e: e.scalar_tensor_tensor(out=dst[:, 0:T - 1], in0=src[:, 1:T], scalar=w2, in1=dst[:, 0:T - 1], op0=ALU.mult, op1=ALU.add),
                    srck + ["PT"] + dstk, dstk)
                dve(lambda e: e.scalar_tensor_tensor(out=tf, in0=src[:, 63:1023:64], scalar=w0, in1=Fb, op0=ALU.mult, op1=ALU.mult),
                    srck + ["PT", "Fb"], ["tf"])
                dve(lambda e: e.tensor_tensor(out=dst[:, 64:1024:64], in0=dst[:, 64:1024:64], in1=tf, op=ALU.subtract), dstk + ["tf"], dstk)
                dve(lambda e: e.scalar_tensor_tensor(out=tf, in0=src[:, 64:1024:64], scalar=w2, in1=Fb, op0=ALU.mult, op1=ALU.mult),
                    srck + ["PT", "Fb"], ["tf"])
                dve(lambda e: e.tensor_tensor(out=dst[:, 63:1023:64], in0=dst[:, 63:1023:64], in1=tf, op=ALU.subtract), dstk + ["tf"], dstk)

            ogT = arena[:, 0:8192].rearrange("p (h t) -> p h t", h=8)
            o_ = [8192]

            def carve(n, dt=BF16):
                a = arena[:, o_[0]:o_[0] + n]
                o_[0] += n
                return a.bitcast(F32) if dt == F32 else a
            HBUF = []
            for _hb in range(2):
                HBUF.append((carve(2048).rearrange("p (t m n) -> p t m n", t=8, m=2),
                             carve(1024).rearrange("p (t n) -> p t n", t=8),
                             carve(1024).rearrange("p (t n) -> p t n", t=8)))
            ZS = [carve(1024) for _ in range(3)]
            oacc = carve(2048, F32)
            qdT = [[carve(1024).rearrange("p (t n) -> p t n", t=8) for _ in range(2)] for _g in range(2)]
            PBd = [[carve(4096).rearrange("p (t q n) -> p t q n", t=8, q=4) for _ in range(2)],
                   [ring[2][:, 0:4096].rearrange("p (t q n) -> p t q n", t=8, q=4),
                    ring[3][:, 0:4096].rearrange("p (t q n) -> p t q n", t=8, q=4)]]
            for d_ in range(2):
                ring_extra[2 + d_] = [K("pb", 1, d_, t_) for t_ in range(8)]
            GS = []
            for g_ in range(2):
                gsd = {}
                gsd["Wt"] = carve(2048).rearrange("p (q s n) -> p q s n", q=4, s=4)
                xy = carve(1024)
                gsd["XY"] = xy.rearrange("p (m q n) -> p m q n", m=2, q=4)
                gsd["PP"] = xy.rearrange("p (q m n) -> p q m n", q=4, m=2)
                gt = carve(2048)
                gsd["GA"] = gt[:, 0:1024].bitcast(F32).rearrange("p (q n) -> p q n", q=4)
                gsd["T1"] = gt[:, 1024:2048].bitcast(F32).rearrange("p (q n) -> p q n", q=4)
                gsd["CC"] = gt.rearrange("p (b m q n) -> p b m q n", b=2, m=2, q=4)
                gsd["BV"] = carve(512).rearrange("p (q n) -> p q n", q=4)
                gsd["BEK"] = carve(512).rearrange("p (q n) -> p q n", q=4)
                GS.append(gsd)
            vnew = [carve(128) for _ in range(2)]
            Sbf = [[carve(128) for _ in range(2)] for _d in range(2)]
            s0h = [carve(256).rearrange("p (d n) -> p d n", d=2) for _ in range(2)]
            So32 = [carve(256, F32) for _ in range(2)]
            assert o_[0] <= arena.shape[1], o_[0]

            colv = lambda qi, t, idx: colsT[:, qi, t * 16 + idx:t * 16 + idx + 1]

            def rows_prep():
                r, rk = wget()
                wv = r[:, 0:256].rearrange("p (c n) -> p c n", c=8)
                psB_, kB_ = ps_pair()
                psA_, kA_ = ps_pair()
                for (ps, kk, c0) in ((psB_, kB_, 0), (psA_, kA_, 16)):
                    def f(e, ps=ps, c0=c0, wv=wv):
                        last = None
                        for half in range(2):
                            for c in range(8):
                                last = e.matmul(ps[0:16, half * 512:(half + 1) * 512], lhsT=wv[:, c, c0:c0 + 16],
                                                rhs=uT[:, c, half * 512:(half + 1) * 512], start=(c == 0), stop=(c == 7))
                        return last
                    pe(f, [rk] + ukeys, kk)
                rtmp = arena[0:16, 19456:33792].bitcast(F32).rearrange("p (a n) -> p a n", a=7)
                rmap = {0: 0, 1: 1, 2: 2, 3: 3, 5: 4, 6: 5, 7: 6}
                R = lambda i: rstd[0:16, :] if i == 4 else rtmp[:, rmap[i], :]
                act(lambda e: e.activation(out=R(0), in_=psB_[0:16, :], func=AF.Sigmoid), kB_, [K("r", 0)])
                act(lambda e: e.activation(out=R(7), in_=psA_[0:16, :], func=AF.Exp, bias=cl[0:16, 2:3]), kA_ + ["cl"], [K("r", 7)])
                yield
                act(lambda e: e.activation(out=R(7), in_=R(7), func=AF.Ln, bias=onec[0:16, 0:1]), [K("r", 7), "onec"], [K("r", 7)])
                act(lambda e: e.activation(out=mx[0:16, 32:33], in_=cl[0:16, 1:2], func=AF.Exp), ["cl"], ["nacol"])
                dve(lambda e: e.tensor_scalar(out=mx[0:16, 32:33], in0=mx[0:16, 32:33], scalar1=-1.0, scalar2=None, op0=ALU.mult), ["nacol"], ["nacol"])
                yield
                dve(lambda e: e.tensor_scalar(out=R(1), in0=R(7), scalar1=mx[0:16, 32:33], scalar2=None, op0=ALU.mult), [K("r", 7), "nacol"], [K("r", 1)])
                dve(lambda e: e.memset(gcb[0:16, :], 1.0), [], ["gcb"])
                dve(lambda e: e.memset(gcb[0:16, 0:1024:64], 0.0), ["gcb"], ["gcb"])
                yield
                dve(lambda e: e.tensor_tensor_scan(out=R(2), data0=gcb[0:16, :], data1=R(1), initial=0.0, op0=ALU.mult, op1=ALU.add),
                    [K("r", 1), "gcb"], [K("r", 2)])
                yield
                pre3 = R(2).rearrange("p (c n) -> p c n", n=64)
                tot_b = pre3[:, :, 63:64].to_broadcast([16, 16, 64])
                v3 = lambda i: R(i).rearrange("p (c n) -> p c n", n=64)
                dve(lambda e: e.tensor_tensor(out=v3(3), in0=tot_b, in1=pre3, op=ALU.subtract), [K("r", 2)], [K("r", 3)])
                yield
                dve(lambda e: e.tensor_tensor(out=R(3), in0=R(3), in1=R(1), op=ALU.add), [K("r", 3), K("r", 1)], [K("r", 3)])
                yield
                dve(lambda e: e.tensor_tensor(out=R(3), in0=R(3), in1=R(2), op=ALU.subtract), [K("r", 3), K("r", 2)], [K("r", 3)])
                yield
                dve(lambda e: e.scalar_tensor_tensor(out=R(4), in0=R(3), scalar=cl[0:16, 6:7], in1=R(2), op0=ALU.mult, op1=ALU.add),
                    [K("r", 3), K("r", 2), "cl"], [K("r", 4)])
                yield
                dve(lambda e: e.tensor_copy(out=gchl[0:16, 0, :], in_=R(4)), [K("r", 4)], ["gchl"])
                act(lambda e: e.activation(out=R(5), in_=R(4), func=AF.Exp), [K("r", 4)], [K("r", 5)])
                yield
                dve(lambda e: e.tensor_tensor(out=gchl[0:16, 1, :], in0=R(4), in1=gchl[0:16, 0, :], op=ALU.subtract), [K("r", 4), "gchl"], ["gchl"])
                yield
                dve(lambda e: e.tensor_tensor(out=R(5), in0=R(5), in1=R(0), op=ALU.mult), [K("r", 5), K("r", 0)], [K("r", 5)])
                yield
                dve(lambda e: e.tensor_tensor(out=v3(6), in0=tot_b, in1=v3(4), op=ALU.subtract), [K("r", 2), K("r", 4)], [K("r", 6)])
                yield
                act(lambda e: e.activation(out=R(6), in_=R(6), func=AF.Exp), [K("r", 6)], [K("r", 6)])
                dve(lambda e: e.tensor_scalar(out=R(7), in0=R(0), scalar1=-1.0, scalar2=None, op0=ALU.mult), [K("r", 0), K("r", 7)], [K("r", 7)])
                yield
                for qi, ri in enumerate((4, 7, 5, 6, 0)):
                    psT_, kT_ = ps_bank()

                    def f(e, ri=ri, psT_=psT_):
                        last = None
                        for t in range(8):
                            last = e.transpose(psT_[:, t * 16:(t + 1) * 16], R(ri)[:, t * 128:(t + 1) * 128], ident32[0:16, 0:16])
                        return last
                    pe(f, [K("r", ri), "cf"], kT_)
                    dve(lambda e, qi=qi, psT_=psT_: e.tensor_copy(out=colsT[:, qi, :], in_=psT_[:, 0:128]), kT_, [K("cols", qi)])
                    yield
                dve(lambda e: e.memset(mx[:, 41:42], 0.0), [],
                    [K("r", i) for i in (0, 1, 2, 3, 5, 6, 7)] + [K("oacc", i) for i in range(16)] + [K("qdT", g_, d_) for g_ in range(2) for d_ in range(2)]
                    + [K("pb", 0, d_, t_) for d_ in range(2) for t_ in range(8)])
                yield

            if SUB <= 1:
                return list(dn_keys)
            mk = {0: (cb[:, 256:384], cb[:, 512:640]), 1: (cb[:, 384:512], cb[:, 640:768])}
            bm16, nmk16, nmk32 = cb[:, 768:896], cb[:, 896:1024], cb[:, 1024:1152]
            def conv_g(src, srck, dst, dstk, wc):
                w0, w1, w2 = (PT[:, c:c + 1] for c in wc)
                act(lambda e: e.activation(out=dst[:], in_=src[:, 0:T], func=AF.Identity, scale=w1), srck + ["PT"], dstk)
                yield
                dve(lambda e: e.scalar_tensor_tensor(out=dst[:, 1:T], in0=src[:, 0:T - 1], scalar=w0, in1=dst[:, 1:T], op0=ALU.mult, op1=ALU.add),
                    srck + ["PT"] + dstk, dstk)
                yield
                dve(lambda e: e.scalar_tensor_tensor(out=dst[:, 0:T - 1], in0=src[:, 1:T], scalar=w2, in1=dst[:, 0:T - 1], op0=ALU.mult, op1=ALU.add),
                    srck + ["PT"] + dstk, dstk)
                yield
                dve(lambda e: e.scalar_tensor_tensor(out=tf, in0=src[:, 63:1023:64], scalar=w0, in1=Fb, op0=ALU.mult, op1=ALU.mult),
                    srck + ["PT", "Fb"], ["tf"])
                dve(lambda e: e.tensor_tensor(out=dst[:, 64:1024:64], in0=dst[:, 64:1024:64], in1=tf, op=ALU.subtract), dstk + ["tf"], dstk)
                dve(lambda e: e.scalar_tensor_tensor(out=tf, in0=src[:, 64:1024:64], scalar=w2, in1=Fb, op0=ALU.mult, op1=ALU.mult),
                    srck + ["PT", "Fb"], ["tf"])
                dve(lambda e: e.tensor_tensor(out=dst[:, 63:1023:64], in0=dst[:, 63:1023:64], in1=tf, op=ALU.subtract), dstk + ["tf"], dstk)
                yield

            def head_prep(h):
                hp = h % 2
                qkT, ktok, vtok = HBUF[hp]
                zs = ZS[h % 3]
                r, rk = wget()
                wv = r[:, 0:4096].rearrange("p (m c n) -> p m c n", m=4, c=8)
                s0k, s1k = [("scr", 0)], [("scr", 1)]
                for m in (1, 0, 2):
                    psQ, kQ = ps_pair()
                    proj(psQ, wv[:, m], (0, 128), kQ, rk)
                    act(lambda e, psQ=psQ: e.activation(out=scr[0][:], in_=psQ[:], func=AF.Identity), kQ, s0k)
                    yield
                    wc = [136 + tap * 24 + m * 8 + h for tap in range(3)]
                    yield from conv_g(scr[0], s0k, scr[1], s1k, wc)
                    act(lambda e: e.activation(out=scr[1][:], in_=scr[1][:], func=AF.Silu), s1k, s1k)
                    yield
                    if m < 2:
                        act(lambda e: e.activation(out=scrb[0][:], in_=scr[1][:], func=AF.Square), s1k, [("scrb", 0)])
                        psN, kN = ps_pair()

                        def f(e, psN=psN):
                            e.matmul(psN[:, 0:512], lhsT=onesb, rhs=scrb[0][:, 0:512], start=True, stop=True)
                            return e.matmul(psN[:, 512:1024], lhsT=onesb, rhs=scrb[0][:, 512:1024], start=True, stop=True)
                        pe(f, [("scrb", 0), "cb"], kN)
                        act(lambda e, psN=psN: e.activation(out=scr[0][:], in_=psN[:], func=AF.Ln, bias=epsc[:, 0:1]), kN + ["epsc"], s0k)
                        yield
                        act(lambda e: e.activation(out=scr[0][:], in_=scr[0][:], func=AF.Exp, scale=-0.5), s0k, s0k)
                        yield
                    if m == 1:
                        dve(lambda e: e.tensor_tensor(out=scr[1][:], in0=scr[1][:], in1=scr[0][:], op=ALU.mult), s0k + s1k, s1k)
                        yield
                        act(lambda e: e.activation(out=qkT[:, :, 0, :], in_=scr[1][:].rearrange("p (t n) -> p t n", t=8), func=AF.Identity),
                            s1k, [K("kT", hp)])
                        yield
                    elif m == 0:
                        dve(lambda e: e.scalar_tensor_tensor(out=qkT[:, :, 1, :], in0=scr[1][:].rearrange("p (t n) -> p t n", t=8), scalar=128.0 ** -0.5,
                                                             in1=scr[0][:].rearrange("p (t n) -> p t n", t=8), op0=ALU.mult, op1=ALU.mult),
                            s0k + s1k, [K("qT", hp)])
                        yield
                    if m >= 1:
                        psT_, kT_ = ps_pair()

                        def f(e, psT_=psT_):
                            last = None
                            for t in range(8):
                                last = e.transpose(psT_[:, t * 128:(t + 1) * 128], scr[1][:, t * 128:(t + 1) * 128], ident32)
                            return last
                        pe(f, s1k + ["cf"], kT_)
                        dst_ = ktok if m == 1 else vtok
                        dve(lambda e, psT_=psT_, dst_=dst_: e.tensor_copy(out=dst_.rearrange("p t n -> p (t n)"), in_=psT_[:]), kT_,
                            [K("ktok" if m == 1 else "vtok", hp)])
                        yield
                psQ, kQ = ps_pair()
                proj(psQ, wv[:, 3], (0, 128), kQ, rk)
                act(lambda e, psQ=psQ: e.activation(out=zs[:], in_=psQ[:], func=AF.Silu), kQ, [K("zs", h % 3)])
                yield

            def dir_common(h, d):
                hp = h % 2
                if d == 1:
                    S.op("pool", lambda e: [e.dma_start(out=s0h[hp][:, 0, :], in_=s0f_in[h]), e.dma_start(out=s0h[hp][:, 1, :], in_=s0b_in[h])],
                         [], [K("s0", hp)], semkey=("ld_s0", hp), n=2, dma=True)
                qkT, ktok, vtok = HBUF[hp]
                zs = ZS[h % 3]
                idx = d * 8 + h
                psGc, kGc = ps_pair()
                sl = sel[0:16, idx % 2, :]
                dve(lambda e: e.tensor_copy(out=sl, in_=identb[0:16, idx:idx + 1].to_broadcast([16, 128])), ["cb"], [("sel", idx % 2)])

                def f(e):
                    e.matmul(psGc[:, 0:512], lhsT=sl, rhs=gchl[0:16, 0, 0:512], start=True, stop=False)
                    e.matmul(psGc[:, 0:512], lhsT=sl, rhs=gchl[0:16, 1, 0:512], start=False, stop=True)
                    e.matmul(psGc[:, 512:1024], lhsT=sl, rhs=gchl[0:16, 0, 512:1024], start=True, stop=False)
                    return e.matmul(psGc[:, 512:1024], lhsT=sl, rhs=gchl[0:16, 1, 512:1024], start=False, stop=True)
                pe(f, [("sel", idx % 2), "gchl"], kGc)
                lastpos = 63 if d == 0 else 0
                act(lambda e: e.activation(out=glc[:, hp * 2 + d, 0:16], in_=psGc[:, lastpos:1024:64], func=AF.Exp), kGc, [K("glc", hp, d)])
                act(lambda e: e.activation(out=qdT[hp][d].rearrange("p t n -> p (t n)"), in_=psGc[:], func=AF.Exp), kGc, [K("qdT", hp, d)])
                dve(lambda e: e.tensor_copy(out=gcb[:], in_=psGc[:]), kGc, ["gcb"])
                dve(lambda e: e.tensor_tensor(out=qdT[hp][d], in0=qkT[:, :, 1, :], in1=qdT[hp][d], op=ALU.mult),
                    [K("qT", hp), K("qdT", hp, d)], [K("qdT", hp, d)])
                yield

            def group_prep(h, d, g):
                hp = h % 2
                qkT, ktok, vtok = HBUF[hp]
                zs = ZS[h % 3]
                idx = d * 8 + h
                maskX, maskA = mk[d]
                t0 = 4 * g
                gs = GS[g]
                Wt, XY, GA, T1, CC, PP, BV, BEK = gs["Wt"], gs["XY"], gs["GA"], gs["T1"], gs["CC"], gs["PP"], gs["BV"], gs["BEK"]
                PB = PBd[hp][d]
                kW = lambda s_: K("Wt", g, s_)
                kXY, kGT, kBB = K("XY", g), K("GT", g), K("BB", g)
                pk = [K("pb", hp, d, t0 + p) for p in range(4)]
                bc = lambda m_: m_.unsqueeze(1).to_broadcast([128, 4, 128])
                col = lambda qi: colsT[:, qi, :].rearrange("p (t i) -> p t i", i=16)[:, t0:t0 + 4, idx:idx + 1].to_broadcast([128, 4, 128])
                v4 = lambda ps, j: ps[:, 0:1024].rearrange("p (q m n) -> p q m n", q=4, m=2)[:, :, j, :]
                dve(lambda e: e.tensor_tensor(out=GA, in0=gcb[:, t0 * 128:(t0 + 4) * 128].rearrange("p (q n) -> p q n", q=4), in1=col(0), op=ALU.subtract),
                    ["gcb", K("cols", 0)], [kGT])
                dve(lambda e: e.scalar_tensor_tensor(out=GA, in0=GA, scalar=-1.0, in1=GA, op0=ALU.mult, op1=ALU.max), [kGT], [kGT])
                act(lambda e: e.activation(out=GA, in_=GA, func=AF.Exp, scale=-1.0), [kGT], [kGT])
                yield
                psK, kK = ps_pair()

                def f(e):
                    last = None
                    for p in range(4):
                        t = t0 + p
                        last = e.matmul(psK[:, p * 256:(p + 1) * 256], lhsT=qkT[:, t, 0, :], rhs=qkT[:, t, :, :].rearrange("p m n -> p (m n)"),
                                        start=True, stop=True)
                    return last
                pe(f, [K("kT", hp), K("qT", hp)], kK)
                dve(lambda e: e.tensor_tensor(out=T1, in0=GA, in1=bc(maskX), op=ALU.mult), [kGT, "cb"], [kGT])
                dve(lambda e: e.tensor_tensor(out=T1, in0=T1, in1=col(1), op=ALU.mult), [kGT, K("cols", 1)], [kGT])
                dve(lambda e: e.tensor_tensor(out=XY[:, 0], in0=v4(psK, 0), in1=T1, op=ALU.mult), kK + [kGT], [kXY])
                dve(lambda e: e.tensor_tensor(out=T1, in0=GA, in1=bc(maskA), op=ALU.mult), [kGT, "cb"], [kGT])
                dve(lambda e: e.tensor_tensor(out=PB[:, t0:t0 + 4, 0, :], in0=v4(psK, 1), in1=T1, op=ALU.mult), kK + [kGT], pk)
                yield
                psY_, kY_ = ps_pair()
                psYb = psY_[:, 0:256].bitcast(BF16).rearrange("p (q n) -> p q n", q=4)

                def f(e):
                    last = None
                    for p in range(4):
                        last = e.transpose(psYb[:, p, :], XY[:, 0, p, :], identb)
                    return last
                pe(f, [kXY, "cb"], kY_[:1])
                act(lambda e: e.activation(out=XY[:, 1], in_=psYb, func=AF.Identity), kY_[:1], [kXY])
                yield
                dve(lambda e: e.tensor_tensor(out=Wt[:, :, 2, :], in0=XY[:, 0], in1=bc(bm16), op=ALU.mult), [kXY, "cb"], [kW(2)])
                dve(lambda e: e.tensor_tensor(out=Wt[:, :, 0, :], in0=XY[:, 1], in1=bc(bm16), op=ALU.mult), [kXY, "cb"], [kW(0)])
                for bi, nm_ in enumerate((nmk16, nmk32)):
                    S.op("pool", lambda e, bi=bi, nm_=nm_: e.tensor_tensor(out=CC[:, bi, 0], in0=XY[:, 0], in1=bc(nm_), op=ALU.mult), [kXY, "cb"], [kGT])
                    S.op("pool", lambda e, bi=bi, nm_=nm_: e.tensor_tensor(out=CC[:, bi, 1], in0=XY[:, 1], in1=bc(nm_), op=ALU.mult), [kXY, "cb"], [kGT])
                act(lambda e: e.activation(out=Wt[:, :, 1, :], in_=bc(identb), func=AF.Identity), ["cb"], [kW(1)])
                act(lambda e: e.activation(out=Wt[:, :, 3, :], in_=bc(identb), func=AF.Identity), ["cb"], [kW(3)])
                yield
                for lv in range(4):
                    psA, kA = ps_pair()
                    psB, kB = ps_pair()
                    last_lv = (lv == 3)

                    def fA(e, psA=psA, last_lv=last_lv):
                        last = None
                        for p in range(4):
                            ap_ = psA[:, p * 256 + 128:p * 256 + 256]
                            if last_lv:
                                e.matmul(ap_, lhsT=Wt[:, p, 2, :], rhs=Wt[:, p, 1, :], start=True, stop=False)
                            else:
                                e.matmul(psA[:, p * 256:p * 256 + 128], lhsT=Wt[:, p, 2, :], rhs=Wt[:, p, 0, :], start=True, stop=True)
                                e.matmul(ap_, lhsT=Wt[:, p, 2, :], rhs=Wt[:, p, 1, :], start=True, stop=False)
                            last = e.matmul(ap_, lhsT=identb, rhs=Wt[:, p, 1, :], start=False, stop=True)
                        return last

                    def fB(e, psB=psB, last_lv=last_lv):
                        last = None
                        for p in range(4):
                            ap_ = psB[:, p * 256 + 128:p * 256 + 256]
                            if last_lv:
                                e.matmul(ap_, lhsT=Wt[:, p, 0, :], rhs=Wt[:, p, 3, :], start=True, stop=False)
                            else:
                                e.matmul(psB[:, p * 256:p * 256 + 128], lhsT=Wt[:, p, 0, :], rhs=Wt[:, p, 2, :], start=True, stop=True)
                                e.matmul(ap_, lhsT=Wt[:, p, 0, :], rhs=Wt[:, p, 3, :], start=True, stop=False)
                            last = e.matmul(ap_, lhsT=identb, rhs=Wt[:, p, 3, :], start=False, stop=True)
                        return last
                    pe(fA, [kW(0), kW(1), kW(2), "cb"], kA)
                    pe(fB, [kW(0), kW(2), kW(3), "cb"], kB)
                    if not last_lv:
                        act(lambda e, psA=psA: e.activation(out=Wt[:, :, 0:2, :].rearrange("p q a n -> p q (a n)"),
                                                            in_=psA[:, 0:1024].rearrange("p (q n) -> p q n", q=4), func=AF.Identity), kA, [kW(0), kW(1)])
                        dve(lambda e, psB=psB: e.tensor_copy(out=Wt[:, :, 2:4, :].rearrange("p q a n -> p q (a n)"),
                                                             in_=psB[:, 0:1024].rearrange("p (q n) -> p q n", q=4)), kB, [kW(2), kW(3)])
                    else:
                        act(lambda e, psA=psA: e.activation(out=Wt[:, :, 1, :], in_=v4(psA, 1), func=AF.Identity), kA, [kW(1)])
                        dve(lambda e, psB=psB: e.tensor_copy(out=Wt[:, :, 3, :], in_=v4(psB, 1)), kB, [kW(3)])
                    yield
                for bi in range(2):
                    psP, kPp = ps_pair()

                    def f(e, psP=psP, bi=bi):
                        last = None
                        for p in range(4):
                            e.matmul(psP[:, p * 256:p * 256 + 128], lhsT=CC[:, bi, 1, p, :], rhs=Wt[:, p, 3, :], start=True, stop=True)
                            last = e.matmul(psP[:, p * 256 + 128:p * 256 + 256], lhsT=CC[:, bi, 0, p, :], rhs=Wt[:, p, 1, :], start=True, stop=True)
                        return last
                    pe(f, [kGT, kW(1), kW(3)], kPp)
                    act(lambda e, psP=psP: e.activation(out=PP.rearrange("p q m n -> p (q m n)"), in_=psP[:, 0:1024], func=AF.Identity, scale=-1.0), kPp, [kXY])
                    yield
                    psL, kL = ps_pair()

                    def f(e, psL=psL):
                        last = None
                        for p in range(4):
                            a0 = psL[:, p * 256:p * 256 + 128]
                            a1 = psL[:, p * 256 + 128:p * 256 + 256]
                            e.matmul(a0, lhsT=Wt[:, p, 3, :], rhs=PP[:, p, 1, :], start=True, stop=False)
                            e.matmul(a0, lhsT=identb, rhs=Wt[:, p, 1, :], start=False, stop=True)
                            e.matmul(a1, lhsT=Wt[:, p, 1, :], rhs=PP[:, p, 0, :], start=True, stop=False)
                            last = e.matmul(a1, lhsT=identb, rhs=Wt[:, p, 3, :], start=False, stop=True)
                        return last
                    pe(f, [kXY, kW(1), kW(3), "cb"], kL)
                    act(lambda e, psL=psL: e.activation(out=Wt[:, :, 1, :], in_=v4(psL, 0), func=AF.Identity), kL, [kW(1)])
                    dve(lambda e, psL=psL: e.tensor_copy(out=Wt[:, :, 3, :], in_=v4(psL, 1)), kL, [kW(3)])
                    yield
                S.op("pool", lambda e: e.tensor_tensor(out=BV, in0=vtok[:, t0:t0 + 4, :], in1=col(4), op=ALU.mult), [K("vtok", hp), K("cols", 4)], [kBB])
                S.op("pool", lambda e: e.tensor_tensor(out=BEK, in0=ktok[:, t0:t0 + 4, :], in1=col(2), op=ALU.mult), [K("ktok", hp), K("cols", 2)], [kBB])
                S.op("pool", lambda e: e.tensor_tensor(out=PB[:, t0:t0 + 4, 1, :], in0=ktok[:, t0:t0 + 4, :], in1=col(3), op=ALU.mult), [K("ktok", hp), K("cols", 3)], pk)
                psU, kU = ps_pair()

                def f(e):
                    last = None
                    for p in range(4):
                        e.matmul(psU[:, p * 256:p * 256 + 128], lhsT=Wt[:, p, 1, :], rhs=BV[:, p, :], start=True, stop=True)
                        last = e.matmul(psU[:, p * 256 + 128:p * 256 + 256], lhsT=BEK[:, p, :], rhs=Wt[:, p, 1, :], start=True, stop=True)
                    return last
                pe(f, [kW(1), kBB], kU)
                act(lambda e: e.activation(out=PB[:, t0:t0 + 4, 2:4, :].rearrange("p q m n -> p q (m n)"),
                                           in_=psU[:, 0:1024].rearrange("p (q n) -> p q n", q=4), func=AF.Identity), kU, pk)
                yield

            def scan(h, d):
                hp = h % 2
                PB = PBd[hp][d]
                si = 0
                cur = s0h[hp][:, d, :]
                curk = K("s0", hp)
                order = range(16) if d == 0 else range(15, -1, -1)
                sout = sf_out if d == 0 else sb_out
                vn = vnew[d]
                vk = K("vn", d)
                for n_, i in enumerate(order):
                    t, cs = i // 2, (i % 2) * 64
                    aT, kd, uu, wT = (PB[:, t, q, :] for q in range(4))
                    pk = K("pb", hp, d, t)
                    psV, kV = ps_bank()
                    pe(lambda e, psV=psV, wT=wT, cs=cs, cur=cur: e.matmul(psV[cs:cs + 64, 0:128], lhsT=wT[:, cs:cs + 64], rhs=cur, start=True, stop=True),
                       [pk, curk], kV)
                    dve(lambda e, psV=psV, uu=uu, cs=cs: e.tensor_tensor(out=vn[cs:cs + 64, :], in0=uu[cs:cs + 64, :], in1=psV[cs:cs + 64, 0:128],
                                                                         op=ALU.subtract), kV + [pk], [vk])
                    yield
                    psO, kO = ps_bank()

                    def f(e, psO=psO, cur=cur, i=i, aT=aT, cs=cs):
                        e.matmul(psO[:, 0:64], lhsT=cur, rhs=qdT[hp][d][:, i // 2, (i % 2) * 64:(i % 2) * 64 + 64], start=True, stop=False)
                        return e.matmul(psO[:, 0:64], lhsT=vn[cs:cs + 64, :], rhs=aT[cs:cs + 64, cs:cs + 64], start=False, stop=True)
                    pe(f, [curk, K("qdT", hp, d), vk, pk], kO)
                    psS, kS = ps_bank()
                    pe(lambda e, psS=psS, kd=kd, cs=cs: e.matmul(psS[:, 0:128], lhsT=kd[cs:cs + 64, :], rhs=vn[cs:cs + 64, :], start=True, stop=True),
                       [pk, vk], kS)
                    first_touch = (i < 8) if d == 0 else (i >= 8)
                    if first_touch:
                        act(lambda e, psO=psO, i=i: e.activation(out=oacc[:, i * 64:(i + 1) * 64], in_=psO[:, 0:64], func=AF.Identity), kO, [K("oacc", i)])
                    else:
                        dve(lambda e, psO=psO, i=i: e.tensor_tensor(out=oacc[:, i * 64:(i + 1) * 64], in0=oacc[:, i * 64:(i + 1) * 64], in1=psO[:, 0:64],
                                                                    op=ALU.add), kO + [K("oacc", i)], [K("oacc", i)])
                    seg_end = (i % 4 == 3) if d == 0 else (i % 4 == 0)
                    nxt = Sbf[d][si % 2]
                    nk = K("Sbf", d, si % 2)
                    si += 1
                    glcol = glc[:, hp * 2 + d, i:i + 1]
                    if not seg_end:
                        dve(lambda e, psS=psS, cur=cur, nxt=nxt, glcol=glcol: e.scalar_tensor_tensor(out=nxt[:], in0=cur, scalar=glcol, in1=psS[:, 0:128],
                                                                                                    op0=ALU.mult, op1=ALU.add),
                            kS + [curk, K("glc", hp, d)], [nk])
                    else:
                        seg = i // 4
                        so = So32[d]
                        sk_ = ("So", d)
                        dve(lambda e, psS=psS, cur=cur, so=so, glcol=glcol: e.scalar_tensor_tensor(out=so[:], in0=cur, scalar=glcol, in1=psS[:, 0:128],
                                                                                                  op0=ALU.mult, op1=ALU.add),
                            kS + [curk, K("glc", hp, d)], [sk_])
                        hw_dma(sout[seg, h], so[:], [sk_], [("sout", d, seg, h)], ("st_s", d))
                        outkeys.append(("sout", d, seg, h))
                        dve(lambda e, so=so, nxt=nxt: e.tensor_scalar(out=nxt[:], in0=so[:], scalar1=cl[:, 5:6], scalar2=None, op0=ALU.mult),
                            [sk_, "cl"], [nk])
                    cur, curk = nxt[:], nk
                    yield

            def head_out(h):
                hp = h % 2
                zs = ZS[h % 3]
                okeys = [K("oacc", i) for i in range(16)]
                act(lambda e: e.activation(out=scrb[1][:], in_=oacc[:], func=AF.Square), okeys, [("scrb", 1)])
                psN, kN = ps_pair()

                def f(e, psN=psN):
                    e.matmul(psN[:, 0:512], lhsT=onesb, rhs=scrb[1][:, 0:512], start=True, stop=True)
                    return e.matmul(psN[:, 512:1024], lhsT=onesb, rhs=scrb[1][:, 512:1024], start=True, stop=True)
                pe(f, [("scrb", 1), "cb"], kN)
                act(lambda e, psN=psN: e.activation(out=rstd[:], in_=psN[:], func=AF.Ln, scale=1.0 / 128, bias=epsc[:, 0:1]), kN + ["epsc"], ["rstd", K("r", 4)])
                yield
                act(lambda e: e.activation(out=rstd[:], in_=rstd[:], func=AF.Exp, scale=-0.5), ["rstd"], ["rstd"])
                dve(lambda e: e.scalar_tensor_tensor(out=rstd[:], in0=oacc[:], scalar=cl[:, 0:1], in1=rstd[:], op0=ALU.mult, op1=ALU.mult),
                    okeys + ["rstd", "cl"], ["rstd"])
                dve(lambda e, h=h: e.tensor_tensor(out=ogT[:, h, :], in0=rstd[:], in1=zs[:], op=ALU.mult), ["rstd", K("zs", h % 3)], [("og", h)])
                yield

            def run(*gens):
                gens = list(gens)
                while gens:
                    for g_ in list(gens):
                        try:
                            next(g_)
                        except StopIteration:
                            gens.remove(g_)

            def chain(*gens):
                for g_ in gens:
                    yield from g_

            def lockstep(*gens):
                gens = list(gens)
                while gens:
                    for g_ in list(gens):
                        try:
                            next(g_)
                        except StopIteration:
                            gens.remove(g_)
                    yield

            def run_bg(mains, bg):
                mains = list(mains)
                while mains:
                    for g_ in list(mains):
                        try:
                            next(g_)
                        except StopIteration:
                            mains.remove(g_)
                    if bg is not None and not bg[1]:
                        try:
                            next(bg[0])
                        except StopIteration:
                            bg[1] = True

            def finish(bg):
                if bg is not None and not bg[1]:
                    for _ in bg[0]:
                        pass
                    bg[1] = True

            ps_mode[0] = "dn"
            tasks = {}
            for h in range(H):
                tasks[("HP", h)] = (lambda h=h: head_prep(h), [("HP", h - 1), ("G1", h - 2), ("SC", h - 3)])
                tasks[("G0", h)] = (lambda h=h: chain(dir_common(h, 0), lockstep(group_prep(h, 0, 0), group_prep(h, 0, 1))),
                                    [("HP", h), ("G1", h - 1), ("SC", h - 2)])
                tasks[("G1", h)] = (lambda h=h: chain(dir_common(h, 1), lockstep(group_prep(h, 1, 0), group_prep(h, 1, 1))), [("G0", h)])
                tasks[("SC", h)] = (lambda h=h: chain(lockstep(scan(h, 0), scan(h, 1)), head_out(h)), [("G1", h), ("SC", h - 1)])
            tasks[("RW", 0)] = (rows_prep, [])
            tasks[("G0", 0)] = (tasks[("G0", 0)][0], tasks[("G0", 0)][1] + [("RW", 0)])
            prio = {"RW": -1, "G0": 0, "G1": 0, "SC": 1, "HP": 2}
            done_t, active = set(), {}
            pending = sorted(tasks.keys(), key=lambda k_: (k_[1], prio[k_[0]]))
            while pending or active:
                for k_ in list(pending):
                    if all((d_ not in tasks) or (d_ in done_t) for d_ in tasks[k_][1]):
                        active[k_] = tasks[k_][0]()
                        pending.remove(k_)
                for k_ in sorted(active.keys(), key=lambda k2: (prio[k2[0]], k2[1])):
                    try:
                        next(active[k_])
                    except StopIteration:
                        del active[k_]
                        done_t.add(k_)
            ps_mode[0] = "all"

            dnk = list(dn_keys)
            zc = arena[:, 8192:16384].rearrange("p (j t) -> p j t", j=8)
            mi = arena[:, 16384:24576].rearrange("p (j t) -> p j t", j=8)
            ogk = [("og", h) for h in range(H)]
            for g in range(4):
                rD, rkD = wget()
                rG, rkG = rD, rkD
                vP_ = rD[:, 0:4096].rearrange("p (m c n) -> p m c n", m=2, c=8)
                vD, vG = vP_[:, 0], vP_[:, 1]
                for jj in range(2):
                    j = g * 2 + jj
                    psG_, kG_ = ps_pair()
                    proj(psG_, vG, (jj * 128, jj * 128 + 128), kG_, rkG)
                    act(lambda e, psG_=psG_: e.activation(out=scrb[0][:], in_=psG_[:], func=AF.Sigmoid), kG_, [("scrb", 0)])
                    psY_, kY_ = ps_pair()

                    def f(e, psY_=psY_, vD=vD, jj=jj):
                        last = None
                        for half in range(2):
                            for c in range(8):
                                last = e.matmul(psY_[:, half * 512:(half + 1) * 512], lhsT=vD[:, c, jj * 128:(jj + 1) * 128],
                                                rhs=ogT[:, c, half * 512:(half + 1) * 512], start=(c == 0), stop=(c == 7))
                        return last
                    pe(f, [rkD] + ogk, kY_)
                    dve(lambda e, psY_=psY_, j=j: e.tensor_tensor(out=mi[:, j, :], in0=scrb[0][:], in1=psY_[:], op=ALU.mult),
                        kY_ + [("scrb", 0)], [("mi", j)] + dnk)
            for j2 in range(4):
                rcx, kcx = wget()
                rbg, kbg = wget()
                vcx = rcx[:, 0:4096].rearrange("p (m c n) -> p m c n", m=2, c=8)
                rs_ = [(rbg, kbg), (rcx, kcx), (rcx, kcx)]
                vs_ = [rbg[:, 0:2048].rearrange("p (c n) -> p c n", c=8), vcx[:, 0], vcx[:, 1]]
                for jj in range(2):
                    j = j2 * 2 + jj
                    cols = (jj * 128, jj * 128 + 128)
                    psC, kC = ps_pair()
                    proj(psC, vs_[1], cols, kC, rs_[1][1])
                    act(lambda e, psC=psC: e.activation(out=scr[0][:], in_=psC[:], func=AF.Identity), kC, [("scr", 0)])
                    psXa, kXa = ps_pair()
                    proj(psXa, vs_[2], cols, kXa, rs_[2][1])
                    dve(lambda e, psXa=psXa: e.tensor_tensor(out=scr[0][:], in0=scr[0][:], in1=psXa[:], op=ALU.mult), kXa + [("scr", 0)], [("scr", 0)])
                    wc = [112 + tap * 8 + j for tap in range(3)]
                    conv(scr[0], [("scr", 0)], scr[1], [("scr", 1)], wc)
                    psB2, kB2 = ps_pair()
                    proj(psB2, vs_[0], cols, kB2, rs_[0][1])
                    dve(lambda e, psB2=psB2, j=j: e.tensor_tensor(out=zc[:, j, :], in0=scr[1][:], in1=psB2[:], op=ALU.mult),
                        kB2 + [("scr", 1)], [("zc", j)] + dnk)
            zck = [("zc", j) for j in range(NT)]
            for g in range(4):
                rD, rkD = wget()
                rG, rkG = rD, rkD
                vP_ = rD[:, 0:4096].rearrange("p (m c n) -> p m c n", m=2, c=8)
                vD, vG = vP_[:, 0], vP_[:, 1]
                for jj in range(2):
                    j = g * 2 + jj
                    psG_, kG_ = ps_pair()
                    proj(psG_, vG, (jj * 128, jj * 128 + 128), kG_, rkG)
                    act(lambda e, psG_=psG_: e.activation(out=scrb[0][:], in_=psG_[:], func=AF.Sigmoid), kG_, [("scrb", 0)])
                    psY_, kY_ = ps_pair()

                    def f(e, psY_=psY_, vD=vD, jj=jj):
                        last = None
                        for half in range(2):
                            for c in range(8):
                                last = e.matmul(psY_[:, half * 512:(half + 1) * 512], lhsT=vD[:, c, jj * 128:(jj + 1) * 128],
                                                rhs=zc[:, c, half * 512:(half + 1) * 512], start=(c == 0), stop=(c == 7))
                        return last
                    pe(f, [rkD] + zck, kY_)
                    dve(lambda e, psY_=psY_: e.tensor_tensor(out=scr[0][:], in0=scrb[0][:], in1=psY_[:], op=ALU.mult), kY_ + [("scrb", 0)], [("scr", 0)])
                    dve(lambda e, j=j: e.tensor_tensor(out=mi[:, j, :], in0=mi[:, j, :], in1=scr[0][:], op=ALU.add), [("scr", 0), ("mi", j)], [("mi", j)])
            mik = [("mi", j) for j in range(NT)]
            for g in range(2):
                rD, rkD = wget()
                vD = rD[:, 0:4096].rearrange("p (c n) -> p c n", c=8)
                for jj in range(4):
                    j = g * 4 + jj
                    psY_, kY_ = ps_pair()

                    def f(e, psY_=psY_, vD=vD, jj=jj):
                        last = None
                        for half in range(2):
                            for c in range(8):
                                last = e.matmul(psY_[:, half * 512:(half + 1) * 512], lhsT=vD[:, c, jj * 128:(jj + 1) * 128],
                                                rhs=mi[:, c, half * 512:(half + 1) * 512], start=(c == 0), stop=(c == 7))
                        return last
                    pe(f, [rkD] + mik, kY_)
                    gc_ = 24 + 8 + j
                    dve(lambda e, j=j, psY_=psY_, gc_=gc_: e.scalar_tensor_tensor(out=xT[:, j, :], in0=psY_[:], scalar=modx[:, gc_:gc_ + 1], in1=xT[:, j, :],
                                                                                 op0=ALU.mult, op1=ALU.add),
                        kY_ + xkeys(j) + [("modg", 1)], xkeys(j))
            return dnk + mik + zck + ogk

        ada_group(0, 6)
        mod_derive(0, 72, 0.5)
        ffn(0, 0, ada_ride=True)
        mod_derive(1, 80, 1.0)
        mod_derive(2, 88, 0.5)
        if STAGE >= 2:
            mkeys = mixer()
            dve(lambda e: e.memset(mx[:, 40:41], 0.0), [], mkeys + [("hT", ft) for ft in range(NFT)])
        ffn(1, 2)

        norm(3)
        yT = arena[:, 0:16384].bitcast(F32).rearrange("p (a n) -> p a n", a=8)
        for j in range(NT):
            dve(lambda e, j=j: e.scalar_tensor_tensor(out=yT[:, j, :], in0=xT[:, j, :], scalar=PT[:, 96 + j:97 + j], in1=rstd[:],
                                                      op0=ALU.mult, op1=ALU.mult),
                xkeys(j) + ["rstd", "PT"], [("yT", j)] + [("hT", ft) for ft in range(NFT)])
        ykeys = [("yT", j) for j in range(NT)]
        for tt in range(8):
            st = scr[tt % 2]
            for dh in range(2):
                psY, kY = ps_pair()

                def f(e, tt=tt, dh=dh, psY=psY):
                    last = None
                    for q in range(4):
                        dj = dh * 4 + q
                        last = e.transpose(psY[:, q * 128:(q + 1) * 128], yT[:, dj, tt * 128:(tt + 1) * 128], ident32)
                    return last
                pe(f, ykeys + ["cf"], kY[:1])
                if dh == 0:
                    dve(lambda e, st=st, psY=psY: e.tensor_copy(out=st[:, 0:512], in_=psY[:, 0:512]), kY[:1], [("scr", tt % 2, 0)] if False else [("scr", tt % 2)])
                else:
                    act(lambda e, st=st, psY=psY: e.activation(out=st[:, 512:1024], in_=psY[:, 0:512], func=AF.Identity), kY[:1], [("scr2", tt % 2)])
            hw_dma(y_out[tt * 128:(tt + 1) * 128, :], st[:], [("scr", tt % 2), ("scr2", tt % 2)], [("yout", tt)], ("st_y", tt))
            S.readers.setdefault(("scr2", tt % 2), [])

        outkeys += [("yout", tt) for tt in range(8)]
        S.op("sp", lambda e: None, outkeys, [])
        S.op("pool", lambda e: None, [("ring", s) for s in range(RING)] + ["cb"] + ([("s0", 0), ("s0", 1)] if STAGE >= 2 else []), [])

        block = es.enter_context(nc.Block())

        @block.tensor
        def _(e):
            S.emit("pe", e)

        @block.scalar
        def _(e):
            S.emit("act", e)

        @block.vector
        def _(e):
            S.emit("dve", e)

        @block.gpsimd
        def _(e):
            S.emit("pool", e)

        @block.sync
        def _(e):
            S.emit("sp", e)
    _DBG["map"] = dbg_map
    return nc


def _consts():
    cf = np.zeros((128, 192), np.float32)
    cf[:, 0:64] = np.arange(64, dtype=np.float32)[None, :]
    cf[:, 64:192] = np.eye(128, dtype=np.float32)
    cb = np.zeros((128, 1152), np.float32)
    cb[:, 0:128] = np.eye(128)
    cb[:, 128:256] = 1.0
    blk = np.zeros((128, 128), np.float32)
    blk[0:64, 0:64] = 1
    blk[64:128, 64:128] = 1
    i = np.arange(128)
    lower = (i[:, None] > i[None, :]).astype(np.float32)
    cb[:, 256:384] = lower * blk
    cb[:, 384:512] = lower.T * blk
    cb[:, 512:640] = (lower.T + np.eye(128)) * blk
    cb[:, 640:768] = (lower + np.eye(128)) * blk
    cb[:, 768:896] = (i[:, None] // 16 == i[None, :] // 16)
    for o_, b_ in ((896, 16), (1024, 32)):
        cb[:, o_:o_ + 128] = -1.0 * ((i[:, None] // (2 * b_) == i[None, :] // (2 * b_)) & (i[:, None] // b_ != i[None, :] // b_))
    return cf, cb


_NC_CACHE = {}


def kernel(x_prompt, x_sample, state_dn_fwd, state_dn_bwd, c, c_ctx, ada_w, ada_b,
           norm_ffn1, ffn1_w_gate, ffn1_w_up, ffn1_w_down, norm_mix, w_in, conv_w,
           conv_out_w, dn_conv_w, dn_a_log, dn_dt_bias, dn_norm_w, dn_out_w, w_o,
           norm_ffn2, ffn2_w_gate, ffn2_w_up, ffn2_w_down, norm_f):
    f32 = np.float32
    A = lambda a: np.ascontiguousarray(np.asarray(a, dtype=f32))
    x_prompt, x_sample = A(x_prompt), A(x_sample)
    cf, cb = _consts()
    shared = {
        "cf": cf, "cb": cb,
        "ada_w": A(ada_w)[0], "ffn1_w_gate": A(ffn1_w_gate)[0], "ffn1_w_up": A(ffn1_w_up)[0], "ffn1_w_down": A(ffn1_w_down)[0],
        "ffn2_w_gate": A(ffn2_w_gate)[0], "ffn2_w_up": A(ffn2_w_up)[0], "ffn2_w_down": A(ffn2_w_down)[0],
        "w_in": A(w_in)[0], "conv_out_w": A(conv_out_w)[0], "dn_out_w": A(dn_out_w)[0], "w_o": A(w_o)[0],
    }
    in_maps = []
    for core in range(8):
        is_sample = core < 2
        if is_sample:
            xc = x_sample[core]
            cond = A(c)[core]
            s0f = A(state_dn_fwd)[core, 0]
            s0b = A(state_dn_bwd)[core, 0]
        else:
            g = (core - 2) % 4
            xc = x_prompt[4 * g:4 * g + 4].reshape(T, D)
            cond = A(c_ctx)
            s0f = np.zeros((H, 128, 128), f32)
            s0b = np.zeros((H, 128, 128), f32)
        pab = np.zeros((256, 128), f32)
        pab[0:72] = A(ada_b)[0].reshape(72, 128)
        pab[72:80] = A(norm_ffn1)[0].reshape(8, 128)
        pab[80:88] = A(norm_mix)[0].reshape(8, 128)
        pab[88:96] = A(norm_ffn2)[0].reshape(8, 128)
        pab[96:104] = A(norm_f).reshape(8, 128)
        pab[104:112] = cond.reshape(8, 128)
        pab[112:136] = A(conv_w)[0].reshape(24, 128)
        pab[136:208] = A(dn_conv_w)[0].reshape(72, 128)
        cl = np.zeros((128, 16), f32)
        cl[:, 0] = A(dn_norm_w)[0]
        cl[0:16, 1] = A(dn_a_log)[0].reshape(16)
        cl[0:16, 2] = A(dn_dt_bias)[0].reshape(16)
        cl[:, 3] = 1.0 if is_sample else 0.0
        cl[:, 4] = 1.0 if is_sample else 0.0
        cl[:, 5] = 1.0 if is_sample else 0.0
        cl[8:16, 6] = 1.0
        cl[:, 7] = np.arange(128, dtype=f32)
        cl[:, 8] = np.arange(128, dtype=f32) + 128
        m = dict(shared)
        m.update({"x": np.ascontiguousarray(xc), "s0f": s0f, "s0b": s0b, "pab": pab, "cl": cl})
        in_maps.append(m)
    if "nc" not in _NC_CACHE:
        _NC_CACHE["nc"] = build_nc()
    nc = _NC_CACHE["nc"]
    res = run_bass_kernel_spmd(nc, in_maps, core_ids=list(range(8)))
    R = res.results
    if DBG:
        _DBG["res"] = R
    y_sample = np.stack([R[0]["y"], R[1]["y"]], axis=0).astype(f32)
    y_prompt = np.concatenate([R[2 + g]["y"].reshape(4, 256, D) for g in range(4)], axis=0).astype(f32)
    nsf = np.concatenate([R[2 + g]["sf"] for g in range(4)], axis=0)[:, None].astype(f32)
    nsb = np.concatenate([R[2 + g]["sb"] for g in range(4)], axis=0)[:, None].astype(f32)
    return (y_prompt, y_sample, nsf, nsb)
```

```python
import os
import math
import numpy as np
from contextlib import ExitStack
import concourse.bass as bass
import concourse.mybir as mybir
from concourse.bass_utils import run_bass_kernel_spmd

F32 = mybir.dt.float32
BF16 = mybir.dt.bfloat16
I32 = mybir.dt.int32
AF = mybir.ActivationFunctionType
ALU = mybir.AluOpType

T = 1024
D = 1024
NT = 8
FF = 2816
NFT = 22
H = 8
NIN = 9248
EPS = 1e-6
STAGE = int(os.environ.get("MK_STAGE", "9"))
SUB = int(os.environ.get("MK_SUB", "9"))
CUT = int(os.environ.get("MK_CUT", "9"))
DBG = int(os.environ.get("MK_DBG", "0"))
_DBG = {}


class Sched:
    ENG = ("pe", "act", "dve", "pool", "sp")

    def __init__(self, nc, es):
        self.nc = nc
        self.es = es
        self.prog = {e: [] for e in self.ENG}
        self.sems = {}
        self.vals = {}
        self.seen = {e: {} for e in self.ENG}
        self.lastw = {}
        self.readers = {}

    def sem(self, key):
        if key not in self.sems:
            self.sems[key] = self.es.enter_context(self.nc.semaphore("s_" + str(key).replace(" ", "")[:40]))
        return self.sems[key]

    def op(self, eng, fn, reads=(), writes=(), semkey=None, n=1, dma=False):
        psr = [k for k in reads if isinstance(k, tuple) and k and k[0] == "ps"]
        if psr:
            reads = [k for k in reads if k not in psr]
            writes = list(writes) + psr
        need = {}
        for k in reads:
            if k in self.lastw:
                sk, v = self.lastw[k]
                need[sk] = max(need.get(sk, 0), v)
        for k in writes:
            if k in self.lastw:
                sk, v = self.lastw[k]
                need[sk] = max(need.get(sk, 0), v)
            for sk, v in self.readers.get(k, ()):
                need[sk] = max(need.get(sk, 0), v)
        waits = []
        for sk, v in need.items():
            if eng == "pe" and sk == "pe":
                continue
            if self.seen[eng].get(sk, 0) >= v:
                continue
            self.seen[eng][sk] = v
            waits.append((sk, v))
        sk = semkey or eng
        unit = 16 if dma else 1
        self.vals[sk] = self.vals.get(sk, 0) + unit * n
        me = (sk, self.vals[sk])
        self.sem(sk)
        self.prog[eng].append((waits, fn, sk, unit))
        for k in reads:
            self.readers.setdefault(k, []).append(me)
        for k in writes:
            self.lastw[k] = me
            self.readers[k] = []
        return me

    def emit(self, eng, e):
        for waits, fn, sk, unit in self.prog[eng]:
            for wk, v in waits:
                e.wait_ge(self.sems[wk], v)
            insts = fn(e)
            if insts is None:
                continue
            if not isinstance(insts, (list, tuple)):
                insts = [insts]
            for i in insts:
                i.then_inc(self.sems[sk], unit)


def build_nc():
    nc = bass.Bass("TRN2", target_bir_lowering=False)

    def din(name, shape):
        return nc.dram_tensor(name, list(shape), F32, kind="ExternalInput").ap()

    def dout(name, shape):
        return nc.dram_tensor(name, list(shape), F32, kind="ExternalOutput").ap()

    x_in = din("x", [T, D])
    s0f_in = din("s0f", [H, 128, 128])
    s0b_in = din("s0b", [H, 128, 128])
    pab_in = din("pab", [256, 128])
    cl_in = din("cl", [128, 16])
    cf_in = din("cf", [128, 192])
    cb_in = din("cb", [128, 9 * 128])
    ada_w = din("ada_w", [D, 9 * D])
    w_g = [din("ffn1_w_gate", [D, FF]), din("ffn2_w_gate", [D, FF])]
    w_u = [din("ffn1_w_up", [D, FF]), din("ffn2_w_up", [D, FF])]
    w_d = [din("ffn1_w_down", [FF, D]), din("ffn2_w_down", [FF, D])]
    w_in = din("w_in", [D, NIN])
    conv_out_w = din("conv_out_w", [D, D])
    dn_out_w = din("dn_out_w", [D, D])
    w_o = din("w_o", [D, D])
    y_out = dout("y", [T, D])
    sf_out = dout("sf", [4, H, 128, 128])
    sb_out = dout("sb", [4, H, 128, 128])
    if DBG:
        dbg32 = dout("dbg32", [128, 8192])
        dbgb = nc.dram_tensor("dbgb", [128, 16384], BF16, kind="ExternalOutput").ap()
    dbg_off = {"f": 0, "b": 0}
    dbg_map = {}

    with ExitStack() as es:
        S = Sched(nc, es)

        def sb(name, shape, dt):
            return es.enter_context(nc.sbuf_tensor("t_" + name, list(shape), dt))

        xT = sb("xT", [128, NT, T], F32)
        uT = sb("uT", [128, NT, T], BF16)
        arena = sb("arena", [128, 47872], BF16)
        RING = 4
        LOOK = 1
        ring = [sb(f"ring{i}", [128, 4096], BF16) for i in range(RING)]
        cf = sb("cf", [128, 192], F32)
        cb = sb("cb", [128, 1152], BF16)
        cl = sb("cl", [128, 16], F32)
        pab = sb("pab", [128, 2, 128], F32)
        PT = sb("PT", [128, 256], F32)
        mod = sb("mod", [128, 72], F32)
        modx = sb("modx", [128, 64], F32)
        scb = sb("scb", [128, NT], BF16)
        sc32 = sb("sc32", [128, NT], F32)
        rstd = sb("rstd", [128, T], F32)
        scr = [sb(f"scr{i}", [128, T], F32) for i in range(2)]
        scrb = [sb(f"scrb{i}", [128, T], BF16) for i in range(2)]

        mx = sb("mx", [128, 64], F32)
        colsT = sb("colsT", [128, 5, 128], F32)
        sel = sb("sel", [16, 2, 128], BF16)
        gchl = sb("gchl", [16, 2, 1024], BF16)
        glc = sb("glc", [128, 4, 16], F32)
        gcb = sb("gcb", [128, T], F32)
        onec = sb("onec", [128, 1], F32)
        so_rr = [0]
        outkeys = []
        ident32 = cf[:, 64:192]
        identb = cb[:, 0:128]
        onesb = cb[:, 128:256]

        pst = [es.enter_context(nc.psum_tensor(f"ps{i}", [128, 1024], F32)) for i in range(4)]
        ps_rr = [0, 0, 0]

        ps_mode = ["all"]

        def ps_pair():
            if ps_mode[0] == "dn":
                i = ps_rr[2] % 3
                ps_rr[2] += 1
            else:
                i = ps_rr[0] % 4
                ps_rr[0] += 1
            return pst[i], [("ps", 2 * i), ("ps", 2 * i + 1)]

        wq = []
        wq_issued = [0]

        DN_LO, DN_HI = 37, 45
        ring_extra = {}

        def look_of(i):
            if STAGE < 2:
                return 1
            if i < DN_LO:
                return 3 if i <= DN_LO - 4 else (2 if i < DN_LO - 1 else 1)
            if i <= DN_HI:
                return 1
            if 50 <= i <= 57:
                return 2
            return 3

        def slot_of(i):
            if STAGE >= 2 and DN_LO <= i <= DN_HI:
                return i % 2
            if STAGE >= 2 and i > DN_HI:
                return (i - DN_HI - 1) % RING
            return i % RING

        def wq_issue_upto(k):
            while wq_issued[0] < min(k, len(wq)):
                i = wq_issued[0]
                slot = slot_of(i)
                pairs = wq[i](ring[slot])

                def fn(e, pairs=pairs):
                    return [e.dma_start(out=d, in_=s) for d, s in pairs]
                S.op("pool", fn, writes=[("ring", slot)] + list(ring_extra.get(slot, ())), semkey=("ringsem", slot), n=len(pairs), dma=True)
                wq_issued[0] += 1

        wq_next = [0]

        def wget():
            i = wq_next[0]
            wq_next[0] += 1
            wq_issue_upto(i + 1 + look_of(i))
            return ring[slot_of(i)], ("ring", slot_of(i))

        def chunk_cols(wap, c0, ncols):
            def mk(r, c0=c0, ncols=ncols):
                v = r[:, 0:8 * ncols].rearrange("p (c n) -> p c n", c=8)
                src = wap.rearrange("(c p) n -> p c n", p=128)
                return [(v[:, c4:c4 + 4, :], src[:, c4:c4 + 4, c0:c0 + ncols]) for c4 in (0, 4)]
            return mk

        def chunk_gu(l, ft2):
            def mk(r):
                v = r[:, 0:4096].rearrange("p (m c n) -> p m c n", m=2, c=8)
                out = []
                for m, wap in ((0, w_g[l]), (1, w_u[l])):
                    src = wap.rearrange("(c p) n -> p c n", p=128)
                    for c in range(8):
                        out.append((v[:, m, c, :], src[:, c, ft2 * 256:(ft2 + 1) * 256]))
                return out
            return mk

        def chunk_down(l, j):
            def mk(r):
                v = r[:, 0:NFT * 128].rearrange("p (f n) -> p f n", f=NFT)
                src = w_d[l].rearrange("(f p) n -> p f n", p=128)
                return [(v[:, f0:f0 + 6, :], src[:, f0:f0 + 6, j * 128:(j + 1) * 128]) for f0 in (0, 6, 12)] + \
                       [(v[:, 18:22, :], src[:, 18:22, j * 128:(j + 1) * 128])]
            return mk

        def chunk_head(h):
            def mk(r):
                v = r[:, 0:4096].rearrange("p (m c n) -> p m c n", m=4, c=8)
                src = w_in.rearrange("(c p) n -> p c n", p=128)
                out = []
                for m in range(4):
                    c0 = 3072 + m * 1024 + h * 128
                    for c0_, c1_ in ((0, 4), (4, 8)):
                        out.append((v[:, m, c0_:c1_, :], src[:, c0_:c1_, c0:c0 + 128]))
                return out
            return mk

        def chunk_ab():
            def mk(r):
                v = r[:, 0:256].rearrange("p (c n) -> p c n", c=8)
                src = w_in.rearrange("(c p) n -> p c n", p=128)
                return [(v, src[:, :, 9216:9248])]
            return mk

        for g in range(6):
            wq.append(chunk_cols(ada_w[:, :], g * 512, 512))
        for ft2 in range(11):
            wq.append(chunk_gu(0, ft2))
            wq.append(chunk_cols(ada_w[:, :], (6 + ft2) * 512, 512))
        wq.append(chunk_down(0, 0))
        wq.append(chunk_cols(ada_w[:, :], 17 * 512, 512))
        for j in range(1, NT):
            wq.append(chunk_down(0, j))
        if STAGE >= 2:
            wq.append(chunk_ab())
            for h in range(H):
                wq.append(chunk_head(h))
            def mk_pair(wa, wb, cb0, g):
                def mk(r):
                    v = r[:, 0:4096].rearrange("p (m c n) -> p m c n", m=2, c=8)
                    sa = wa.rearrange("(c p) n -> p c n", p=128)
                    sb_ = wb.rearrange("(c p) n -> p c n", p=128)
                    return [(v[:, 0], sa[:, :, g * 256:(g + 1) * 256]), (v[:, 1], sb_[:, :, cb0 + g * 256:cb0 + (g + 1) * 256])]
                return mk
            for g in range(4):
                wq.append(mk_pair(dn_out_w, w_in, 8192, g))
            for j2 in range(4):
                def mk_cx(r, j2=j2):
                    v = r[:, 0:4096].rearrange("p (m c n) -> p m c n", m=2, c=8)
                    src = w_in.rearrange("(c p) n -> p c n", p=128)
                    return [(v[:, m_], src[:, :, (1 + m_) * 1024 + j2 * 256:(1 + m_) * 1024 + (j2 + 1) * 256]) for m_ in range(2)]
                wq.append(mk_cx)
                wq.append(chunk_cols(w_in, j2 * 256, 256))
            for g in range(4):
                wq.append(mk_pair(conv_out_w, w_in, 7168, g))
            for g in range(2):
                wq.append(chunk_cols(w_o, g * 512, 512))
        for ft2 in range(11):
            wq.append(chunk_gu(1, ft2))
        for j in range(NT):
            wq.append(chunk_down(1, j))

        def dve(fn, reads, writes):
            return S.op("dve", fn, reads, writes)

        def act(fn, reads, writes):
            return S.op("act", fn, reads, writes)

        def pe(fn, reads, writes):
            return S.op("pe", fn, reads, writes)

        def hw_dma(out, in_, reads, writes, semkey):
            return S.op("sp", lambda e: e.dma_start(out=out, in_=in_), reads, writes, semkey=semkey, dma=True)

        def dbg(name, ap, reads, n=None):
            if not DBG or name in dbg_map:
                return
            is32 = ap.dtype == F32
            kind = "f" if is32 else "b"
            npart, nfree = ap.shape[0], int(np.prod(ap.shape[1:]))
            o = dbg_off[kind]
            dbg_off[kind] += nfree
            dbg_map[name] = (kind, o, npart, tuple(ap.shape[1:]))
            dst = (dbg32 if is32 else dbgb)[0:npart, o:o + nfree]
            if len(ap.shape) == 3:
                dst = dst.rearrange("p (a n) -> p a n", a=ap.shape[1])
            hw_dma(dst, ap, reads, [("dbgout", name)], ("dbg", len(dbg_map) % 8))
            outkeys.append(("dbgout", name))

        hw_dma(cf[:], cf_in[:, :], [], ["cf"], "ld_c0")
        hw_dma(cl[:], cl_in[:, :], [], ["cl"], "ld_c1")
        hw_dma(pab[:], pab_in.rearrange("(a p) n -> p a n", p=128), [], ["pab"], "ld_c2")
        S.op("pool", lambda e: e.dma_start(out=cb[:], in_=cb_in[:, :]), [], ["cb"], semkey="ld_cb", dma=True)

        psP, kP = ps_pair()

        def f(e):
            e.transpose(psP[:, 0:128], pab[:, 0, :], ident32)
            return e.transpose(psP[:, 128:256], pab[:, 1, :], ident32)
        pe(f, ["pab", "cf"], kP)
        dve(lambda e: e.tensor_copy(out=PT[:], in_=psP[:, 0:256]), kP, ["PT"])
        act(lambda e: e.activation(out=sc32[:], in_=PT[:, 104:112], func=AF.Silu), ["PT"], ["sc32"])
        dve(lambda e: e.tensor_copy(out=scb[:], in_=sc32[:]), ["sc32"], ["scb"])

        def ada_group(g0, g1):
            psM, kM = ps_pair()
            for g in range(g0, g1):
                r, rk = wget()
                v = r[:, 0:4096].rearrange("p (c n) -> p c n", c=8)

                def f(e, v=v, g=g):
                    last = None
                    for t4 in range(4):
                        col = g * 4 + t4
                        for c in range(8):
                            last = e.matmul(psM[:, col:col + 1], lhsT=v[:, c, t4 * 128:(t4 + 1) * 128], rhs=scb[:, c:c + 1],
                                            start=(c == 0), stop=(c == 7))
                    return last
                pe(f, [rk, "scb"], kM)
            c0, c1 = g0 * 4, g1 * 4
            dve(lambda e: e.tensor_tensor(out=mod[:, c0:c1], in0=psM[:, c0:c1], in1=PT[:, c0:c1], op=ALU.add), kM + ["PT"], [("mod", g0)])

        def ada_chunk(g):
            r, rk = wget()
            v = r[:, 0:4096].rearrange("p (c n) -> p c n", c=8)
            psM, kM = ps_pair()

            def f(e, v=v, psM=psM):
                last = None
                for t4 in range(4):
                    for c in range(8):
                        last = e.matmul(psM[:, t4:t4 + 1], lhsT=v[:, c, t4 * 128:(t4 + 1) * 128], rhs=scb[:, c:c + 1], start=(c == 0), stop=(c == 7))
                return last
            pe(f, [rk, "scb"], kM[:1])
            c0 = g * 4
            dve(lambda e, psM=psM, c0=c0: e.tensor_tensor(out=mod[:, c0:c0 + 4], in0=psM[:, 0:4], in1=PT[:, c0:c0 + 4], op=ALU.add),
                kM[:1] + ["PT"], [("mod", 6 if g < 12 else 12)])

        def mod_derive(k, normcol, gscale):
            sh, sc_, gt = 3 * k, 3 * k + 1, 3 * k + 2
            dve(lambda e: e.scalar_tensor_tensor(out=modx[:, k * 8:k * 8 + 8], in0=mod[:, sc_ * 8:sc_ * 8 + 8], scalar=1.0,
                                                 in1=PT[:, normcol:normcol + 8], op0=ALU.add, op1=ALU.mult),
                [("mod", 0), ("mod", 6), ("mod", 12), "PT"], [("modx", k)])
            dve(lambda e: e.tensor_scalar(out=modx[:, 24 + k * 8:24 + k * 8 + 8], in0=mod[:, gt * 8:gt * 8 + 8], scalar1=gscale, scalar2=None,
                                          op0=ALU.mult),
                [("mod", 0), ("mod", 6), ("mod", 12)], [("modg", k)])

        xs = arena[:, 0:16384].bitcast(F32).rearrange("p (a n) -> p a n", a=8)
        psg = arena[:, 16384:26624]
        for tt in range(8):
            hw_dma(xs[:, tt, :], x_in[tt * 128:(tt + 1) * 128, :], [], [("xs", tt)], ("ld_x", tt))
        for half in range(2):
            for dj in range(NT):
                psX, kX = ps_pair()

                def f(e, dj=dj, half=half, psX=psX):
                    last = None
                    for q in range(4):
                        tt = half * 4 + q
                        last = e.transpose(psX[:, q * 128:(q + 1) * 128], xs[:, tt, dj * 128:(dj + 1) * 128], ident32)
                    return last
                pe(f, [("xs", half * 4 + q) for q in range(4)] + ["cf"], kX[:1])
                dst = xT[:, dj, half * 512:(half + 1) * 512]
                if dj % 2 == 0:
                    dve(lambda e, dst=dst, psX=psX: e.tensor_copy(out=dst, in_=psX[:, 0:512]), kX[:1], [("xT", dj, half)])
                else:
                    act(lambda e, dst=dst, psX=psX: e.activation(out=dst, in_=psX[:, 0:512], func=AF.Identity), kX[:1], [("xT", dj, half)])
        pose = arena[:, 40000:41616].bitcast(F32)
        pe_om = pose[:, 0:2]
        act(lambda e: e.activation(out=pe_om, in_=cl[:, 7:9], func=AF.Exp, scale=-math.log(10000.0) / 256.0), ["cl"], ["pe_om"])
        ANG = pose[:, 8:168].rearrange("p (j n) -> p j n", j=2)
        PS_ = pose[:, 168:328].rearrange("p (j n) -> p j n", j=2)
        PC_ = pose[:, 328:488].rearrange("p (j n) -> p j n", j=2)
        PT1 = pose[:, 488:648].rearrange("p (j n) -> p j n", j=2)
        PT2 = pose[:, 648:808].rearrange("p (j n) -> p j n", j=2)
        for jj in range(2):
            dve(lambda e, jj=jj: e.tensor_scalar(out=ANG[:, jj, 0:16], in0=cf[:, 0:16], scalar1=pose[:, jj:jj + 1], scalar2=None, op0=ALU.mult),
                ["cf", "pe_om"], ["pe_ang"])
            dve(lambda e, jj=jj: e.tensor_scalar(out=ANG[:, jj, 16:80], in0=cf[:, 0:64], scalar1=pose[:, jj:jj + 1], scalar2=None, op0=ALU.mult),
                ["cf", "pe_om"], ["pe_ang"])
        dve(lambda e: e.memset(pose[:, 4:5], math.pi / 2), [], ["pe_hpi"])
        act(lambda e: e.activation(out=PS_, in_=ANG, func=AF.Sin, scale=1.0 / 32), ["pe_ang"], ["pe_s"])
        act(lambda e: e.activation(out=PC_, in_=ANG, func=AF.Sin, scale=-1.0 / 32, bias=pose[:, 4:5]), ["pe_ang", "pe_hpi"], ["pe_c"])
        for it in range(5):
            dve(lambda e: e.tensor_tensor(out=PT1, in0=PC_, in1=PC_, op=ALU.mult), ["pe_c"], ["pe_t1"])
            dve(lambda e: e.tensor_tensor(out=PT2, in0=PS_, in1=PS_, op=ALU.mult), ["pe_s"], ["pe_t2"])
            dve(lambda e: e.scalar_tensor_tensor(out=PS_, in0=PS_, scalar=2.0, in1=PC_, op0=ALU.mult, op1=ALU.mult), ["pe_s", "pe_c", "pe_t2"], ["pe_s"])
            dve(lambda e: e.tensor_tensor(out=PC_, in0=PT1, in1=PT2, op=ALU.subtract), ["pe_t1", "pe_t2", "pe_s"], ["pe_c"])
        dve(lambda e: e.tensor_scalar(out=PS_, in0=PS_, scalar1=cl[:, 3:4], scalar2=None, op0=ALU.mult), ["pe_s", "cl"], ["pe_s"])
        dve(lambda e: e.tensor_scalar(out=PC_, in0=PC_, scalar1=cl[:, 3:4], scalar2=None, op0=ALU.mult), ["pe_c", "cl"], ["pe_c"])
        for j in range(NT):
            tab = PS_ if (j // 2) % 2 == 0 else PC_
            jj = j % 2
            if j < 4:
                src = tab[:, jj, 0:16].unsqueeze(2).to_broadcast([128, 16, 64])
            else:
                src = tab[:, jj, 16:80].unsqueeze(1).to_broadcast([128, 16, 64])
            xv = xT[:, j, :].rearrange("p (r c) -> p r c", r=16)
            dve(lambda e, xv=xv, src=src: e.tensor_tensor(out=xv, in0=xv, in1=src, op=ALU.add),
                ["pe_s", "pe_c", ("xT", j, 0), ("xT", j, 1)], [("xT", j, 0), ("xT", j, 1)])

        def xkeys(j):
            return [("xT", j, 0), ("xT", j, 1)]

        def norm(k):
            psN, kN = ps_pair()
            for j in range(NT):
                sq = scrb[j % 2]
                act(lambda e, j=j, sq=sq: e.activation(out=sq[:], in_=xT[:, j, :], func=AF.Square), xkeys(j), [("scrb", j % 2)])

                def f(e, j=j, sq=sq):
                    e.matmul(psN[:, 0:512], lhsT=onesb, rhs=sq[:, 0:512], start=(j == 0), stop=(j == NT - 1))
                    return e.matmul(psN[:, 512:1024], lhsT=onesb, rhs=sq[:, 512:1024], start=(j == 0), stop=(j == NT - 1))
                pe(f, [("scrb", j % 2), "cb"], kN)
            act(lambda e: e.activation(out=rstd[:], in_=psN[:], func=AF.Ln, scale=1.0 / D, bias=epsc[:, 0:1]), kN + ["epsc"], ["rstd"])
            act(lambda e: e.activation(out=rstd[:], in_=rstd[:], func=AF.Exp, scale=-0.5), ["rstd"], ["rstd"])

        epsc = sb("epsc", [128, 1], F32)
        dve(lambda e: e.memset(epsc[:], EPS), [], ["epsc"])
        dve(lambda e: e.memset(onec[:], 1.0), [], ["onec"])

        def norm_apply(k):
            for j in range(NT):
                tmp = scr[j % 2]
                dve(lambda e, j=j, tmp=tmp: e.scalar_tensor_tensor(out=tmp[:], in0=xT[:, j, :], scalar=modx[:, k * 8 + j:k * 8 + j + 1],
                                                                   in1=rstd[:], op0=ALU.mult, op1=ALU.mult),
                    xkeys(j) + ["rstd", ("modx", k)], [("scr", j % 2)])
                shc = (3 * k) * 8 + j
                act(lambda e, j=j, tmp=tmp, shc=shc: e.activation(out=uT[:, j, :], in_=tmp[:], func=AF.Identity, bias=mod[:, shc:shc + 1]),
                    [("scr", j % 2), ("mod", 0), ("mod", 6), ("mod", 12)], [("uT", j)])

        hT = arena[:, 0:NFT * T].rearrange("p (f t) -> p f t", f=NFT)

        def ffn(l, k, ada_ride=False):
            norm(k)
            norm_apply(k)
            ukeys = [("uT", j) for j in range(NT)]
            for ft2 in range(11):
                if ada_ride and ft2 > 0:
                    ada_chunk(5 + ft2)
                r, rk = wget()
                v = r[:, 0:4096].rearrange("p (m c n) -> p m c n", m=2, c=8)
                for fi in range(2):
                    ft = ft2 * 2 + fi
                    psG, kG = ps_pair()
                    psU, kU = ps_pair()

                    def mm_gu(e, v=v, fi=fi, ps=psG, m=0):
                        last = None
                        for half in range(2):
                            for c in range(8):
                                last = e.matmul(ps[:, half * 512:(half + 1) * 512], lhsT=v[:, m, c, fi * 128:(fi + 1) * 128],
                                                rhs=uT[:, c, half * 512:(half + 1) * 512], start=(c == 0), stop=(c == 7))
                        return last
                    pe(mm_gu, [rk] + ukeys, kG)
                    pe(lambda e, g=mm_gu, v=v, fi=fi, ps=psU: g(e, v, fi, ps, 1), [rk] + ukeys, kU)
                    sg = scrb[ft % 2]
                    act(lambda e, sg=sg, psG=psG: e.activation(out=sg[:], in_=psG[:], func=AF.Silu), kG, [("scrb", ft % 2)])
                    dve(lambda e, sg=sg, psU=psU, ft=ft: e.tensor_tensor(out=hT[:, ft, :], in0=sg[:], in1=psU[:], op=ALU.mult),
                        kU + [("scrb", ft % 2)], [("hT", ft)])
            hkeys = [("hT", ft) for ft in range(NFT)]
            for j in range(NT):
                if ada_ride and j == 0:
                    ada_chunk(16)
                if ada_ride and j == 1:
                    ada_chunk(17)
                r, rk = wget()
                v = r[:, 0:NFT * 128].rearrange("p (f n) -> p f n", f=NFT)
                psD, kD = ps_pair()

                def f(e, v=v, psD=psD):
                    last = None
                    for half in range(2):
                        for ft in range(NFT):
                            last = e.matmul(psD[:, half * 512:(half + 1) * 512], lhsT=v[:, ft, :], rhs=hT[:, ft, half * 512:(half + 1) * 512],
                                            start=(ft == 0), stop=(ft == NFT - 1))
                    return last
                pe(f, [rk] + hkeys, kD)
                gc_ = 24 + k * 8 + j
                dve(lambda e, j=j, psD=psD, gc_=gc_: e.scalar_tensor_tensor(out=xT[:, j, :], in0=psD[:], scalar=modx[:, gc_:gc_ + 1], in1=xT[:, j, :],
                                                                            op0=ALU.mult, op1=ALU.add),
                    kD + xkeys(j) + [("modg", k)], xkeys(j))

        def mixer():
            norm(1)
            norm_apply(1)
            ukeys = [("uT", j) for j in range(NT)]
            dn_keys = set()

            def K(*k):
                dn_keys.add(k)
                return k

            def ps_bank():
                i = (6 + ps_rr[1] % 2) if ps_mode[0] == "dn" else ps_rr[1] % 8
                ps_rr[1] += 1
                return pst[i // 2][:, (i % 2) * 512:(i % 2) * 512 + 512], [("ps", i)]

            def proj(ps, wv, cols, keys_w, rk):
                def f(e):
                    last = None
                    for half in range(2):
                        for c in range(8):
                            last = e.matmul(ps[:, half * 512:(half + 1) * 512], lhsT=wv[:, c, cols[0]:cols[1]],
                                            rhs=uT[:, c, half * 512:(half + 1) * 512], start=(c == 0), stop=(c == 7))
                    return last
                pe(f, [rk] + ukeys, keys_w)

            Fb = mx[:, 0:15]
            dve(lambda e: e.tensor_copy(out=Fb, in_=cl[:, 4:5].to_broadcast([128, 15])), ["cl"], ["Fb"])
            dve(lambda e: e.memset(mx[:, 3:15:4], 1.0), ["Fb"], ["Fb"])
            tf = mx[:, 16:31]

            def conv(src, srck, dst, dstk, wc):
                w0, w1, w2 = (PT[:, c:c + 1] for c in wc)
                act(lambda e: e.activation(out=dst[:], in_=src[:, 0:T], func=AF.Identity, scale=w1), srck + ["PT"], dstk)
                dve(lambda e: e.scalar_tensor_tensor(out=dst[:, 1:T], in0=src[:, 0:T - 1], scalar=w0, in1=dst[:, 1:T], op0=ALU.mult, op1=ALU.add),
                    srck + ["PT"] + dstk, dstk)
                dve(lambda e: e.scalar_tensor_tensor(out=dst[:, 0:T - 1], in0=src[:, 1:T], scalar=w2, in1=dst[:, 0:T - 1], op0=ALU.mult, op1=ALU.add),
                    srck + ["PT"] + dstk, dstk)
                dve(lambda e: e.scalar_tensor_tensor(out=tf, in0=src[:, 63:1023:64], scalar=w0, in1=Fb, op0=ALU.mult, op1=ALU.mult),
                    srck + ["PT", "Fb"], ["tf"])
                dve(lambda e: e.tensor_tensor(out=dst[:, 64:1024:64], in0=dst[:, 64:1024:64], in1=tf, op=ALU.subtract), dstk + ["tf"], dstk)
                dve(lambda e: e.scalar_tensor_tensor(out=tf, in0=src[:, 64:1024:64], scalar=w2, in1=Fb, op0=ALU.mult, op1=ALU.mult),
                    srck + ["PT", "Fb"], ["tf"])
                dve(lambda e: e.tensor_tensor(out=dst[:, 63:1023:64], in0=dst[:, 63:1023:64], in1=tf, op=ALU.subtract), dstk + ["tf"], dstk)

            ogT = arena[:, 0:8192].rearrange("p (h t) -> p h t", h=8)
            o_ = [8192]

            def carve(n, dt=BF16):
                a = arena[:, o_[0]:o_[0] + n]
                o_[0] += n
                return a.bitcast(F32) if dt == F32 else a
            HBUF = []
            for _hb in range(2):
                HBUF.append((carve(2048).rearrange("p (t m n) -> p t m n", t=8, m=2),
                             carve(1024).rearrange("p (t n) -> p t n", t=8),
                             carve(1024).rearrange("p (t n) -> p t n", t=8)))
            ZS = [carve(1024) for _ in range(3)]
            oacc = carve(2048, F32)
            qdT = [[carve(1024).rearrange("p (t n) -> p t n", t=8) for _ in range(2)] for _g in range(2)]
            PBd = [[carve(4096).rearrange("p (t q n) -> p t q n", t=8, q=4) for _ in range(2)],
                   [ring[2][:, 0:4096].rearrange("p (t q n) -> p t q n", t=8, q=4),
                    ring[3][:, 0:4096].rearrange("p (t q n) -> p t q n", t=8, q=4)]]
            for d_ in range(2):
                ring_extra[2 + d_] = [K("pb", 1, d_, t_) for t_ in range(8)]
            GS = []
            for g_ in range(2):
                gsd = {}
                gsd["Wt"] = carve(2048).rearrange("p (q s n) -> p q s n", q=4, s=4)
                xy = carve(1024)
                gsd["XY"] = xy.rearrange("p (m q n) -> p m q n", m=2, q=4)
                gsd["PP"] = xy.rearrange("p (q m n) -> p q m n", q=4, m=2)
                gt = carve(2048)
                gsd["GA"] = gt[:, 0:1024].bitcast(F32).rearrange("p (q n) -> p q n", q=4)
                gsd["T1"] = gt[:, 1024:2048].bitcast(F32).rearrange("p (q n) -> p q n", q=4)
                gsd["CC"] = gt.rearrange("p (b m q n) -> p b m q n", b=2, m=2, q=4)
                gsd["BV"] = carve(512).rearrange("p (q n) -> p q n", q=4)
                gsd["BEK"] = carve(512).rearrange("p (q n) -> p q n", q=4)
                GS.append(gsd)
            vnew = [carve(128) for _ in range(2)]
            Sbf = [[carve(128) for _ in range(2)] for _d in range(2)]
            s0h = [carve(256).rearrange("p (d n) -> p d n", d=2) for _ in range(2)]
            So32 = [carve(256, F32) for _ in range(2)]
            assert o_[0] <= arena.shape[1], o_[0]

            colv = lambda qi, t, idx: colsT[:, qi, t * 16 + idx:t * 16 + idx + 1]

            def rows_prep():
                r, rk = wget()
                wv = r[:, 0:256].rearrange("p (c n) -> p c n", c=8)
                psB_, kB_ = ps_pair()
                psA_, kA_ = ps_pair()
                for (ps, kk, c0) in ((psB_, kB_, 0), (psA_, kA_, 16)):
                    def f(e, ps=ps, c0=c0, wv=wv):
                        last = None
                        for half in range(2):
                            for c in range(8):
                                last = e.matmul(ps[0:16, half * 512:(half + 1) * 512], lhsT=wv[:, c, c0:c0 + 16],
                                                rhs=uT[:, c, half * 512:(half + 1) * 512], start=(c == 0), stop=(c == 7))
                        return last
                    pe(f, [rk] + ukeys, kk)
                rtmp = arena[0:16, 19456:33792].bitcast(F32).rearrange("p (a n) -> p a n", a=7)
                rmap = {0: 0, 1: 1, 2: 2, 3: 3, 5: 4, 6: 5, 7: 6}
                R = lambda i: rstd[0:16, :] if i == 4 else rtmp[:, rmap[i], :]
                act(lambda e: e.activation(out=R(0), in_=psB_[0:16, :], func=AF.Sigmoid), kB_, [K("r", 0)])
                act(lambda e: e.activation(out=R(7), in_=psA_[0:16, :], func=AF.Exp, bias=cl[0:16, 2:3]), kA_ + ["cl"], [K("r", 7)])
                yield
                act(lambda e: e.activation(out=R(7), in_=R(7), func=AF.Ln, bias=onec[0:16, 0:1]), [K("r", 7), "onec"], [K("r", 7)])
                act(lambda e: e.activation(out=mx[0:16, 32:33], in_=cl[0:16, 1:2], func=AF.Exp), ["cl"], ["nacol"])
                dve(lambda e: e.tensor_scalar(out=mx[0:16, 32:33], in0=mx[0:16, 32:33], scalar1=-1.0, scalar2=None, op0=ALU.mult), ["nacol"], ["nacol"])
                yield
                dve(lambda e: e.tensor_scalar(out=R(1), in0=R(7), scalar1=mx[0:16, 32:33], scalar2=None, op0=ALU.mult), [K("r", 7), "nacol"], [K("r", 1)])
                dve(lambda e: e.memset(gcb[0:16, :], 1.0), [], ["gcb"])
                dve(lambda e: e.memset(gcb[0:16, 0:1024:64], 0.0), ["gcb"], ["gcb"])
                yield
                dve(lambda e: e.tensor_tensor_scan(out=R(2), data0=gcb[0:16, :], data1=R(1), initial=0.0, op0=ALU.mult, op1=ALU.add),
                    [K("r", 1), "gcb"], [K("r", 2)])
                yield
                pre3 = R(2).rearrange("p (c n) -> p c n", n=64)
                tot_b = pre3[:, :, 63:64].to_broadcast([16, 16, 64])
                v3 = lambda i: R(i).rearrange("p (c n) -> p c n", n=64)
                dve(lambda e: e.tensor_tensor(out=v3(3), in0=tot_b, in1=pre3, op=ALU.subtract), [K("r", 2)], [K("r", 3)])
                yield
                dve(lambda e: e.tensor_tensor(out=R(3), in0=R(3), in1=R(1), op=ALU.add), [K("r", 3), K("r", 1)], [K("r", 3)])
                yield
                dve(lambda e: e.tensor_tensor(out=R(3), in0=R(3), in1=R(2), op=ALU.subtract), [K("r", 3), K("r", 2)], [K("r", 3)])
                yield
                dve(lambda e: e.scalar_tensor_tensor(out=R(4), in0=R(3), scalar=cl[0:16, 6:7], in1=R(2), op0=ALU.mult, op1=ALU.add),
                    [K("r", 3), K("r", 2), "cl"], [K("r", 4)])
                yield
                dve(lambda e: e.tensor_copy(out=gchl[0:16, 0, :], in_=R(4)), [K("r", 4)], ["gchl"])
                act(lambda e: e.activation(out=R(5), in_=R(4), func=AF.Exp), [K("r", 4)], [K("r", 5)])
                yield
                dve(lambda e: e.tensor_tensor(out=gchl[0:16, 1, :], in0=R(4), in1=gchl[0:16, 0, :], op=ALU.subtract), [K("r", 4), "gchl"], ["gchl"])
                yield
                dve(lambda e: e.tensor_tensor(out=R(5), in0=R(5), in1=R(0), op=ALU.mult), [K("r", 5), K("r", 0)], [K("r", 5)])
                yield
                dve(lambda e: e.tensor_tensor(out=v3(6), in0=tot_b, in1=v3(4), op=ALU.subtract), [K("r", 2), K("r", 4)], [K("r", 6)])
                yield
                act(lambda e: e.activation(out=R(6), in_=R(6), func=AF.Exp), [K("r", 6)], [K("r", 6)])
                dve(lambda e: e.tensor_scalar(out=R(7), in0=R(0), scalar1=-1.0, scalar2=None, op0=ALU.mult), [K("r", 0), K("r", 7)], [K("r", 7)])
                yield
                for qi, ri in enumerate((4, 7, 5, 6, 0)):
                    psT_, kT_ = ps_bank()

                    def f(e, ri=ri, psT_=psT_):
                        last = None
                        for t in range(8):
                            last = e.transpose(psT_[:, t * 16:(t + 1) * 16], R(ri)[:, t * 128:(t + 1) * 128], ident32[0:16, 0:16])
                        return last
                    pe(f, [K("r", ri), "cf"], kT_)
                    dve(lambda e, qi=qi, psT_=psT_: e.tensor_copy(out=colsT[:, qi, :], in_=psT_[:, 0:128]), kT_, [K("cols", qi)])
                    yield
                dve(lambda e: e.memset(mx[:, 41:42], 0.0), [],
                    [K("r", i) for i in (0, 1, 2, 3, 5, 6, 7)] + [K("oacc", i) for i in range(16)] + [K("qdT", g_, d_) for g_ in range(2) for d_ in range(2)]
                    + [K("pb", 0, d_, t_) for d_ in range(2) for t_ in range(8)])
                yield

            if SUB <= 1:
                return list(dn_keys)
            mk = {0: (cb[:, 256:384], cb[:, 512:640]), 1: (cb[:, 384:512], cb[:, 640:768])}
            bm16, nmk16, nmk32 = cb[:, 768:896], cb[:, 896:1024], cb[:, 1024:1152]
            def conv_g(src, srck, dst, dstk, wc):
                w0, w1, w2 = (PT[:, c:c + 1] for c in wc)
                act(lambda e: e.activation(out=dst[:], in_=src[:, 0:T], func=AF.Identity, scale=w1), srck + ["PT"], dstk)
                yield
                dve(lambda e: e.scalar_tensor_tensor(out=dst[:, 1:T], in0=src[:, 0:T - 1], scalar=w0, in1=dst[:, 1:T], op0=ALU.mult, op1=ALU.add),
                    srck + ["PT"] + dstk, dstk)
                yield
                dve(lambda e: e.scalar_tensor_tensor(out=dst[:, 0:T - 1], in0=src[:, 1:T], scalar=w2, in1=dst[:, 0:T - 1], op0=ALU.mult, op1=ALU.add),
                    srck + ["PT"] + dstk, dstk)
                yield
                dve(lambda e: e.scalar_tensor_tensor(out=tf, in0=src[:, 63:1023:64], scalar=w0, in1=Fb, op0=ALU.mult, op1=ALU.mult),
                    srck + ["PT", "Fb"], ["tf"])
                dve(lambda e: e.tensor_tensor(out=dst[:, 64:1024:64], in0=dst[:, 64:1024:64], in1=tf, op=ALU.subtract), dstk + ["tf"], dstk)
                dve(lambda e: e.scalar_tensor_tensor(out=tf, in0=src[:, 64:1024:64], scalar=w2, in1=Fb, op0=ALU.mult, op1=ALU.mult),
                    srck + ["PT", "Fb"], ["tf"])
                dve(lambda e: e.tensor_tensor(out=dst[:, 63:1023:64], in0=dst[:, 63:1023:64], in1=tf, op=ALU.subtract), dstk + ["tf"], dstk)
                yield

            def head_prep(h):
                hp = h % 2
                qkT, ktok, vtok = HBUF[hp]
                zs = ZS[h % 3]
                r, rk = wget()
                wv = r[:, 0:4096].rearrange("p (m c n) -> p m c n", m=4, c=8)
                s0k, s1k = [("scr", 0)], [("scr", 1)]
                for m in (1, 0, 2):
                    psQ, kQ = ps_pair()
                    proj(psQ, wv[:, m], (0, 128), kQ, rk)
                    act(lambda e, psQ=psQ: e.activation(out=scr[0][:], in_=psQ[:], func=AF.Identity), kQ, s0k)
                    yield
                    wc = [136 + tap * 24 + m * 8 + h for tap in range(3)]
                    yield from conv_g(scr[0], s0k, scr[1], s1k, wc)
                    act(lambda e: e.activation(out=scr[1][:], in_=scr[1][:], func=AF.Silu), s1k, s1k)
                    yield
                    if m < 2:
                        act(lambda e: e.activation(out=scrb[0][:], in_=scr[1][:], func=AF.Square), s1k, [("scrb", 0)])
                        psN, kN = ps_pair()

                        def f(e, psN=psN):
                            e.matmul(psN[:, 0:512], lhsT=onesb, rhs=scrb[0][:, 0:512], start=True, stop=True)
                            return e.matmul(psN[:, 512:1024], lhsT=onesb, rhs=scrb[0][:, 512:1024], start=True, stop=True)
                        pe(f, [("scrb", 0), "cb"], kN)
                        act(lambda e, psN=psN: e.activation(out=scr[0][:], in_=psN[:], func=AF.Ln, bias=epsc[:, 0:1]), kN + ["epsc"], s0k)
                        yield
                        act(lambda e: e.activation(out=scr[0][:], in_=scr[0][:], func=AF.Exp, scale=-0.5), s0k, s0k)
                        yield
                    if m == 1:
                        dve(lambda e: e.tensor_tensor(out=scr[1][:], in0=scr[1][:], in1=scr[0][:], op=ALU.mult), s0k + s1k, s1k)
                        yield
                        act(lambda e: e.activation(out=qkT[:, :, 0, :], in_=scr[1][:].rearrange("p (t n) -> p t n", t=8), func=AF.Identity),
                            s1k, [K("kT", hp)])
                        yield
                    elif m == 0:
                        dve(lambda e: e.scalar_tensor_tensor(out=qkT[:, :, 1, :], in0=scr[1][:].rearrange("p (t n) -> p t n", t=8), scalar=128.0 ** -0.5,
                                                             in1=scr[0][:].rearrange("p (t n) -> p t n", t=8), op0=ALU.mult, op1=ALU.mult),
                            s0k + s1k, [K("qT", hp)])
                        yield
                    if m >= 1:
                        psT_, kT_ = ps_pair()

                        def f(e, psT_=psT_):
                            last = None
                            for t in range(8):
                                last = e.transpose(psT_[:, t * 128:(t + 1) * 128], scr[1][:, t * 128:(t + 1) * 128], ident32)
                            return last
                        pe(f, s1k + ["cf"], kT_)
                        dst_ = ktok if m == 1 else vtok
                        dve(lambda e, psT_=psT_, dst_=dst_: e.tensor_copy(out=dst_.rearrange("p t n -> p (t n)"), in_=psT_[:]), kT_,
                            [K("ktok" if m == 1 else "vtok", hp)])
                        yield
                psQ, kQ = ps_pair()
                proj(psQ, wv[:, 3], (0, 128), kQ, rk)
                act(lambda e, psQ=psQ: e.activation(out=zs[:], in_=psQ[:], func=AF.Silu), kQ, [K("zs", h % 3)])
                yield

            def dir_common(h, d):
                hp = h % 2
                if d == 1:
                    S.op("pool", lambda e: [e.dma_start(out=s0h[hp][:, 0, :], in_=s0f_in[h]), e.dma_start(out=s0h[hp][:, 1, :], in_=s0b_in[h])],
                         [], [K("s0", hp)], semkey=("ld_s0", hp), n=2, dma=True)
                qkT, ktok, vtok = HBUF[hp]
                zs = ZS[h % 3]
                idx = d * 8 + h
                psGc, kGc = ps_pair()
                sl = sel[0:16, idx % 2, :]
                dve(lambda e: e.tensor_copy(out=sl, in_=identb[0:16, idx:idx + 1].to_broadcast([16, 128])), ["cb"], [("sel", idx % 2)])

                def f(e):
                    e.matmul(psGc[:, 0:512], lhsT=sl, rhs=gchl[0:16, 0, 0:512], start=True, stop=False)
                    e.matmul(psGc[:, 0:512], lhsT=sl, rhs=gchl[0:16, 1, 0:512], start=False, stop=True)
                    e.matmul(psGc[:, 512:1024], lhsT=sl, rhs=gchl[0:16, 0, 512:1024], start=True, stop=False)
                    return e.matmul(psGc[:, 512:1024], lhsT=sl, rhs=gchl[0:16, 1, 512:1024], start=False, stop=True)
                pe(f, [("sel", idx % 2), "gchl"], kGc)
                lastpos = 63 if d == 0 else 0
                act(lambda e: e.activation(out=glc[:, hp * 2 + d, 0:16], in_=psGc[:, lastpos:1024:64], func=AF.Exp), kGc, [K("glc", hp, d)])
                act(lambda e: e.activation(out=qdT[hp][d].rearrange("p t n -> p (t n)"), in_=psGc[:], func=AF.Exp), kGc, [K("qdT", hp, d)])
                dve(lambda e: e.tensor_copy(out=gcb[:], in_=psGc[:]), kGc, ["gcb"])
                dve(lambda e: e.tensor_tensor(out=qdT[hp][d], in0=qkT[:, :, 1, :], in1=qdT[hp][d], op=ALU.mult),
                    [K("qT", hp), K("qdT", hp, d)], [K("qdT", hp, d)])
                yield

            def group_prep(h, d, g):
                hp = h % 2
                qkT, ktok, vtok = HBUF[hp]
                zs = ZS[h % 3]
                idx = d * 8 + h
                maskX, maskA = mk[d]
                t0 = 4 * g
                gs = GS[g]
                Wt, XY, GA, T1, CC, PP, BV, BEK = gs["Wt"], gs["XY"], gs["GA"], gs["T1"], gs["CC"], gs["PP"], gs["BV"], gs["BEK"]
                PB = PBd[hp][d]
                kW = lambda s_: K("Wt", g, s_)
                kXY, kGT, kBB = K("XY", g), K("GT", g), K("BB", g)
                pk = [K("pb", hp, d, t0 + p) for p in range(4)]
                bc = lambda m_: m_.unsqueeze(1).to_broadcast([128, 4, 128])
                col = lambda qi: colsT[:, qi, :].rearrange("p (t i) -> p t i", i=16)[:, t0:t0 + 4, idx:idx + 1].to_broadcast([128, 4, 128])
                v4 = lambda ps, j: ps[:, 0:1024].rearrange("p (q m n) -> p q m n", q=4, m=2)[:, :, j, :]
                dve(lambda e: e.tensor_tensor(out=GA, in0=gcb[:, t0 * 128:(t0 + 4) * 128].rearrange("p (q n) -> p q n", q=4), in1=col(0), op=ALU.subtract),
                    ["gcb", K("cols", 0)], [kGT])
                dve(lambda e: e.scalar_tensor_tensor(out=GA, in0=GA, scalar=-1.0, in1=GA, op0=ALU.mult, op1=ALU.max), [kGT], [kGT])
                act(lambda e: e.activation(out=GA, in_=GA, func=AF.Exp, scale=-1.0), [kGT], [kGT])
                yield
                psK, kK = ps_pair()

                def f(e):
                    last = None
                    for p in range(4):
                        t = t0 + p
                        last = e.matmul(psK[:, p * 256:(p + 1) * 256], lhsT=qkT[:, t, 0, :], rhs=qkT[:, t, :, :].rearrange("p m n -> p (m n)"),
                                        start=True, stop=True)
                    return last
                pe(f, [K("kT", hp), K("qT", hp)], kK)
                dve(lambda e: e.tensor_tensor(out=T1, in0=GA, in1=bc(maskX), op=ALU.mult), [kGT, "cb"], [kGT])
                dve(lambda e: e.tensor_tensor(out=T1, in0=T1, in1=col(1), op=ALU.mult), [kGT, K("cols", 1)], [kGT])
                dve(lambda e: e.tensor_tensor(out=XY[:, 0], in0=v4(psK, 0), in1=T1, op=ALU.mult), kK + [kGT], [kXY])
                dve(lambda e: e.tensor_tensor(out=T1, in0=GA, in1=bc(maskA), op=ALU.mult), [kGT, "cb"], [kGT])
                dve(lambda e: e.tensor_tensor(out=PB[:, t0:t0 + 4, 0, :], in0=v4(psK, 1), in1=T1, op=ALU.mult), kK + [kGT], pk)
                yield
                psY_, kY_ = ps_pair()
                psYb = psY_[:, 0:256].bitcast(BF16).rearrange("p (q n) -> p q n", q=4)

                def f(e):
                    last = None
                    for p in range(4):
                        last = e.transpose(psYb[:, p, :], XY[:, 0, p, :], identb)
                    return last
                pe(f, [kXY, "cb"], kY_[:1])
                act(lambda e: e.activation(out=XY[:, 1], in_=psYb, func=AF.Identity), kY_[:1], [kXY])
                yield
                dve(lambda e: e.tensor_tensor(out=Wt[:, :, 2, :], in0=XY[:, 0], in1=bc(bm16), op=ALU.mult), [kXY, "cb"], [kW(2)])
                dve(lambda e: e.tensor_tensor(out=Wt[:, :, 0, :], in0=XY[:, 1], in1=bc(bm16), op=ALU.mult), [kXY, "cb"], [kW(0)])
                for bi, nm_ in enumerate((nmk16, nmk32)):
                    S.op("pool", lambda e, bi=bi, nm_=nm_: e.tensor_tensor(out=CC[:, bi, 0], in0=XY[:, 0], in1=bc(nm_), op=ALU.mult), [kXY, "cb"], [kGT])
                    S.op("pool", lambda e, bi=bi, nm_=nm_: e.tensor_tensor(out=CC[:, bi, 1], in0=XY[:, 1], in1=bc(nm_), op=ALU.mult), [kXY, "cb"], [kGT])
                act(lambda e: e.activation(out=Wt[:, :, 1, :], in_=bc(identb), func=AF.Identity), ["cb"], [kW(1)])
                act(lambda e: e.activation(out=Wt[:, :, 3, :], in_=bc(identb), func=AF.Identity), ["cb"], [kW(3)])
                yield
                for lv in range(4):
                    psA, kA = ps_pair()
                    psB, kB = ps_pair()
                    last_lv = (lv == 3)

                    def fA(e, psA=psA, last_lv=last_lv):
                        last = None
                        for p in range(4):
                            ap_ = psA[:, p * 256 + 128:p * 256 + 256]
                            if last_lv:
                                e.matmul(ap_, lhsT=Wt[:, p, 2, :], rhs=Wt[:, p, 1, :], start=True, stop=False)
                            else:
                                e.matmul(psA[:, p * 256:p * 256 + 128], lhsT=Wt[:, p, 2, :], rhs=Wt[:, p, 0, :], start=True, stop=True)
                                e.matmul(ap_, lhsT=Wt[:, p, 2, :], rhs=Wt[:, p, 1, :], start=True, stop=False)
                            last = e.matmul(ap_, lhsT=identb, rhs=Wt[:, p, 1, :], start=False, stop=True)
                        return last

                    def fB(e, psB=psB, last_lv=last_lv):
                        last = None
                        for p in range(4):
                            ap_ = psB[:, p * 256 + 128:p * 256 + 256]
                            if last_lv:
                                e.matmul(ap_, lhsT=Wt[:, p, 0, :], rhs=Wt[:, p, 3, :], start=True, stop=False)
                            else:
                                e.matmul(psB[:, p * 256:p * 256 + 128], lhsT=Wt[:, p, 0, :], rhs=Wt[:, p, 2, :], start=True, stop=True)
                                e.matmul(ap_, lhsT=Wt[:, p, 0, :], rhs=Wt[:, p, 3, :], start=True, stop=False)
                            last = e.matmul(ap_, lhsT=identb, rhs=Wt[:, p, 3, :], start=False, stop=True)
                        return last
                    pe(fA, [kW(0), kW(1), kW(2), "cb"], kA)
                    pe(fB, [kW(0), kW(2), kW(3), "cb"], kB)
                    if not last_lv:
                        act(lambda e, psA=psA: e.activation(out=Wt[:, :, 0:2, :].rearrange("p q a n -> p q (a n)"),
                                                            in_=psA[:, 0:1024].rearrange("p (q n) -> p q n", q=4), func=AF.Identity), kA, [kW(0), kW(1)])
                        dve(lambda e, psB=psB: e.tensor_copy(out=Wt[:, :, 2:4, :].rearrange("p q a n -> p q (a n)"),
                                                             in_=psB[:, 0:1024].rearrange("p (q n) -> p q n", q=4)), kB, [kW(2), kW(3)])
                    else:
                        act(lambda e, psA=psA: e.activation(out=Wt[:, :, 1, :], in_=v4(psA, 1), func=AF.Identity), kA, [kW(1)])
                        dve(lambda e, psB=psB: e.tensor_copy(out=Wt[:, :, 3, :], in_=v4(psB, 1)), kB, [kW(3)])
                    yield
                for bi in range(2):
                    psP, kPp = ps_pair()

                    def f(e, psP=psP, bi=bi):
                        last = None
                        for p in range(4):
                            e.matmul(psP[:, p * 256:p * 256 + 128], lhsT=CC[:, bi, 1, p, :], rhs=Wt[:, p, 3, :], start=True, stop=True)
                            last = e.matmul(psP[:, p * 256 + 128:p * 256 + 256], lhsT=CC[:, bi, 0, p, :], rhs=Wt[:, p, 1, :], start=True, stop=True)
                        return last
                    pe(f, [kGT, kW(1), kW(3)], kPp)
                    act(lambda e, psP=psP: e.activation(out=PP.rearrange("p q m n -> p (q m n)"), in_=psP[:, 0:1024], func=AF.Identity, scale=-1.0), kPp, [kXY])
                    yield
                    psL, kL = ps_pair()

                    def f(e, psL=psL):
                        last = None
                        for p in range(4):
                            a0 = psL[:, p * 256:p * 256 + 128]
                            a1 = psL[:, p * 256 + 128:p * 256 + 256]
                            e.matmul(a0, lhsT=Wt[:, p, 3, :], rhs=PP[:, p, 1, :], start=True, stop=False)
                            e.matmul(a0, lhsT=identb, rhs=Wt[:, p, 1, :], start=False, stop=True)
                            e.matmul(a1, lhsT=Wt[:, p, 1, :], rhs=PP[:, p, 0, :], start=True, stop=False)
                            last = e.matmul(a1, lhsT=identb, rhs=Wt[:, p, 3, :], start=False, stop=True)
                        return last
                    pe(f, [kXY, kW(1), kW(3), "cb"], kL)
                    act(lambda e, psL=psL: e.activation(out=Wt[:, :, 1, :], in_=v4(psL, 0), func=AF.Identity), kL, [kW(1)])
                    dve(lambda e, psL=psL: e.tensor_copy(out=Wt[:, :, 3, :], in_=v4(psL, 1)), kL, [kW(3)])
                    yield
                S.op("pool", lambda e: e.tensor_tensor(out=BV, in0=vtok[:, t0:t0 + 4, :], in1=col(4), op=ALU.mult), [K("vtok", hp), K("cols", 4)], [kBB])
                S.op("pool", lambda e: e.tensor_tensor(out=BEK, in0=ktok[:, t0:t0 + 4, :], in1=col(2), op=ALU.mult), [K("ktok", hp), K("cols", 2)], [kBB])
                S.op("pool", lambda e: e.tensor_tensor(out=PB[:, t0:t0 + 4, 1, :], in0=ktok[:, t0:t0 + 4, :], in1=col(3), op=ALU.mult), [K("ktok", hp), K("cols", 3)], pk)
                psU, kU = ps_pair()

                def f(e):
                    last = None
                    for p in range(4):
                        e.matmul(psU[:, p * 256:p * 256 + 128], lhsT=Wt[:, p, 1, :], rhs=BV[:, p, :], start=True, stop=True)
                        last = e.matmul(psU[:, p * 256 + 128:p * 256 + 256], lhsT=BEK[:, p, :], rhs=Wt[:, p, 1, :], start=True, stop=True)
                    return last
                pe(f, [kW(1), kBB], kU)
                act(lambda e: e.activation(out=PB[:, t0:t0 + 4, 2:4, :].rearrange("p q m n -> p q (m n)"),
                                           in_=psU[:, 0:1024].rearrange("p (q n) -> p q n", q=4), func=AF.Identity), kU, pk)
                yield

            def scan(h, d):
                hp = h % 2
                PB = PBd[hp][d]
                si = 0
                cur = s0h[hp][:, d, :]
                curk = K("s0", hp)
                order = range(16) if d == 0 else range(15, -1, -1)
                sout = sf_out if d == 0 else sb_out
                vn = vnew[d]
                vk = K("vn", d)
                for n_, i in enumerate(order):
                    t, cs = i // 2, (i % 2) * 64
                    aT, kd, uu, wT = (PB[:, t, q, :] for q in range(4))
                    pk = K("pb", hp, d, t)
                    psV, kV = ps_bank()
                    pe(lambda e, psV=psV, wT=wT, cs=cs, cur=cur: e.matmul(psV[cs:cs + 64, 0:128], lhsT=wT[:, cs:cs + 64], rhs=cur, start=True, stop=True),
                       [pk, curk], kV)
                    dve(lambda e, psV=psV, uu=uu, cs=cs: e.tensor_tensor(out=vn[cs:cs + 64, :], in0=uu[cs:cs + 64, :], in1=psV[cs:cs + 64, 0:128],
                                                                         op=ALU.subtract), kV + [pk], [vk])
                    yield
                    psO, kO = ps_bank()

                    def f(e, psO=psO, cur=cur, i=i, aT=aT, cs=cs):
                        e.matmul(psO[:, 0:64], lhsT=cur, rhs=qdT[hp][d][:, i // 2, (i % 2) * 64:(i % 2) * 64 + 64], start=True, stop=False)
                        return e.matmul(psO[:, 0:64], lhsT=vn[cs:cs + 64, :], rhs=aT[cs:cs + 64, cs:cs + 64], start=False, stop=True)
                    pe(f, [curk, K("qdT", hp, d), vk, pk], kO)
                    psS, kS = ps_bank()
                    pe(lambda e, psS=psS, kd=kd, cs=cs: e.matmul(psS[:, 0:128], lhsT=kd[cs:cs + 64, :], rhs=vn[cs:cs + 64, :], start=True, stop=True),
                       [pk, vk], kS)
                    first_touch = (i < 8) if d == 0 else (i >= 8)
                    if first_touch:
                        act(lambda e, psO=psO, i=i: e.activation(out=oacc[:, i * 64:(i + 1) * 64], in_=psO[:, 0:64], func=AF.Identity), kO, [K("oacc", i)])
                    else:
                        dve(lambda e, psO=psO, i=i: e.tensor_tensor(out=oacc[:, i * 64:(i + 1) * 64], in0=oacc[:, i * 64:(i + 1) * 64], in1=psO[:, 0:64],
                                                                    op=ALU.add), kO + [K("oacc", i)], [K("oacc", i)])
                    seg_end = (i % 4 == 3) if d == 0 else (i % 4 == 0)
                    nxt = Sbf[d][si % 2]
                    nk = K("Sbf", d, si % 2)
                    si += 1
                    glcol = glc[:, hp * 2 + d, i:i + 1]
                    if not seg_end:
                        dve(lambda e, psS=psS, cur=cur, nxt=nxt, glcol=glcol: e.scalar_tensor_tensor(out=nxt[:], in0=cur, scalar=glcol, in1=psS[:, 0:128],
                                                                                                    op0=ALU.mult, op1=ALU.add),
                            kS + [curk, K("glc", hp, d)], [nk])
                    else:
                        seg = i // 4
                        so = So32[d]
                        sk_ = ("So", d)
                        dve(lambda e, psS=psS, cur=cur, so=so, glcol=glcol: e.scalar_tensor_tensor(out=so[:], in0=cur, scalar=glcol, in1=psS[:, 0:128],
                                                                                                  op0=ALU.mult, op1=ALU.add),
                            kS + [curk, K("glc", hp, d)], [sk_])
                        hw_dma(sout[seg, h], so[:], [sk_], [("sout", d, seg, h)], ("st_s", d))
                        outkeys.append(("sout", d, seg, h))
                        dve(lambda e, so=so, nxt=nxt: e.tensor_scalar(out=nxt[:], in0=so[:], scalar1=cl[:, 5:6], scalar2=None, op0=ALU.mult),
                            [sk_, "cl"], [nk])
                    cur, curk = nxt[:], nk
                    yield

            def head_out(h):
                hp = h % 2
                zs = ZS[h % 3]
                okeys = [K("oacc", i) for i in range(16)]
                act(lambda e: e.activation(out=scrb[1][:], in_=oacc[:], func=AF.Square), okeys, [("scrb", 1)])
                psN, kN = ps_pair()

                def f(e, psN=psN):
                    e.matmul(psN[:, 0:512], lhsT=onesb, rhs=scrb[1][:, 0:512], start=True, stop=True)
                    return e.matmul(psN[:, 512:1024], lhsT=onesb, rhs=scrb[1][:, 512:1024], start=True, stop=True)
                pe(f, [("scrb", 1), "cb"], kN)
                act(lambda e, psN=psN: e.activation(out=rstd[:], in_=psN[:], func=AF.Ln, scale=1.0 / 128, bias=epsc[:, 0:1]), kN + ["epsc"], ["rstd", K("r", 4)])
                yield
                act(lambda e: e.activation(out=rstd[:], in_=rstd[:], func=AF.Exp, scale=-0.5), ["rstd"], ["rstd"])
                dve(lambda e: e.scalar_tensor_tensor(out=rstd[:], in0=oacc[:], scalar=cl[:, 0:1], in1=rstd[:], op0=ALU.mult, op1=ALU.mult),
                    okeys + ["rstd", "cl"], ["rstd"])
                dve(lambda e, h=h: e.tensor_tensor(out=ogT[:, h, :], in0=rstd[:], in1=zs[:], op=ALU.mult), ["rstd", K("zs", h % 3)], [("og", h)])
                yield

            def run(*gens):
                gens = list(gens)
                while gens:
                    for g_ in list(gens):
                        try:
                            next(g_)
                        except StopIteration:
                            gens.remove(g_)

            def chain(*gens):
                for g_ in gens:
                    yield from g_

            def lockstep(*gens):
                gens = list(gens)
                while gens:
                    for g_ in list(gens):
                        try:
                            next(g_)
                        except StopIteration:
                            gens.remove(g_)
                    yield

            def run_bg(mains, bg):
                mains = list(mains)
                while mains:
                    for g_ in list(mains):
                        try:
                            next(g_)
                        except StopIteration:
                            mains.remove(g_)
                    if bg is not None and not bg[1]:
                        try:
                            next(bg[0])
                        except StopIteration:
                            bg[1] = True

            def finish(bg):
                if bg is not None and not bg[1]:
                    for _ in bg[0]:
                        pass
                    bg[1] = True

            ps_mode[0] = "dn"
            tasks = {}
            for h in range(H):
                tasks[("HP", h)] = (lambda h=h: head_prep(h), [("HP", h - 1), ("G1", h - 2), ("SC", h - 3)])
                tasks[("G0", h)] = (lambda h=h: chain(dir_common(h, 0), lockstep(group_prep(h, 0, 0), group_prep(h, 0, 1))),
                                    [("HP", h), ("G1", h - 1), ("SC", h - 2)])
                tasks[("G1", h)] = (lambda h=h: chain(dir_common(h, 1), lockstep(group_prep(h, 1, 0), group_prep(h, 1, 1))), [("G0", h)])
                tasks[("SC", h)] = (lambda h=h: chain(lockstep(scan(h, 0), scan(h, 1)), head_out(h)), [("G1", h), ("SC", h - 1)])
            tasks[("RW", 0)] = (rows_prep, [])
            tasks[("G0", 0)] = (tasks[("G0", 0)][0], tasks[("G0", 0)][1] + [("RW", 0)])
            prio = {"RW": -1, "G0": 0, "G1": 0, "SC": 1, "HP": 2}
            done_t, active = set(), {}
            pending = sorted(tasks.keys(), key=lambda k_: (k_[1], prio[k_[0]]))
            while pending or active:
                for k_ in list(pending):
                    if all((d_ not in tasks) or (d_ in done_t) for d_ in tasks[k_][1]):
                        active[k_] = tasks[k_][0]()
                        pending.remove(k_)
                for k_ in sorted(active.keys(), key=lambda k2: (prio[k2[0]], k2[1])):
                    try:
                        next(active[k_])
                    except StopIteration:
                        del active[k_]
                        done_t.add(k_)
            ps_mode[0] = "all"

            dnk = list(dn_keys)
            zc = arena[:, 8192:16384].rearrange("p (j t) -> p j t", j=8)
            mi = arena[:, 16384:24576].rearrange("p (j t) -> p j t", j=8)
            ogk = [("og", h) for h in range(H)]
            for g in range(4):
                rD, rkD = wget()
                rG, rkG = rD, rkD
                vP_ = rD[:, 0:4096].rearrange("p (m c n) -> p m c n", m=2, c=8)
                vD, vG = vP_[:, 0], vP_[:, 1]
                for jj in range(2):
                    j = g * 2 + jj
                    psG_, kG_ = ps_pair()
                    proj(psG_, vG, (jj * 128, jj * 128 + 128), kG_, rkG)
                    act(lambda e, psG_=psG_: e.activation(out=scrb[0][:], in_=psG_[:], func=AF.Sigmoid), kG_, [("scrb", 0)])
                    psY_, kY_ = ps_pair()

                    def f(e, psY_=psY_, vD=vD, jj=jj):
                        last = None
                        for half in range(2):
                            for c in range(8):
                                last = e.matmul(psY_[:, half * 512:(half + 1) * 512], lhsT=vD[:, c, jj * 128:(jj + 1) * 128],
                                                rhs=ogT[:, c, half * 512:(half + 1) * 512], start=(c == 0), stop=(c == 7))
                        return last
                    pe(f, [rkD] + ogk, kY_)
                    dve(lambda e, psY_=psY_, j=j: e.tensor_tensor(out=mi[:, j, :], in0=scrb[0][:], in1=psY_[:], op=ALU.mult),
                        kY_ + [("scrb", 0)], [("mi", j)] + dnk)
            for j2 in range(4):
                rcx, kcx = wget()
                rbg, kbg = wget()
                vcx = rcx[:, 0:4096].rearrange("p (m c n) -> p m c n", m=2, c=8)
                rs_ = [(rbg, kbg), (rcx, kcx), (rcx, kcx)]
                vs_ = [rbg[:, 0:2048].rearrange("p (c n) -> p c n", c=8), vcx[:, 0], vcx[:, 1]]
                for jj in range(2):
                    j = j2 * 2 + jj
                    cols = (jj * 128, jj * 128 + 128)
                    psC, kC = ps_pair()
                    proj(psC, vs_[1], cols, kC, rs_[1][1])
                    act(lambda e, psC=psC: e.activation(out=scr[0][:], in_=psC[:], func=AF.Identity), kC, [("scr", 0)])
                    psXa, kXa = ps_pair()
                    proj(psXa, vs_[2], cols, kXa, rs_[2][1])
                    dve(lambda e, psXa=psXa: e.tensor_tensor(out=scr[0][:], in0=scr[0][:], in1=psXa[:], op=ALU.mult), kXa + [("scr", 0)], [("scr", 0)])
                    wc = [112 + tap * 8 + j for tap in range(3)]
                    conv(scr[0], [("scr", 0)], scr[1], [("scr", 1)], wc)
                    psB2, kB2 = ps_pair()
                    proj(psB2, vs_[0], cols, kB2, rs_[0][1])
                    dve(lambda e, psB2=psB2, j=j: e.tensor_tensor(out=zc[:, j, :], in0=scr[1][:], in1=psB2[:], op=ALU.mult),
                        kB2 + [("scr", 1)], [("zc", j)] + dnk)
            zck = [("zc", j) for j in range(NT)]
            for g in range(4):
                rD, rkD = wget()
                rG, rkG = rD, rkD
                vP_ = rD[:, 0:4096].rearrange("p (m c n) -> p m c n", m=2, c=8)
                vD, vG = vP_[:, 0], vP_[:, 1]
                for jj in range(2):
                    j = g * 2 + jj
                    psG_, kG_ = ps_pair()
                    proj(psG_, vG, (jj * 128, jj * 128 + 128), kG_, rkG)
                    act(lambda e, psG_=psG_: e.activation(out=scrb[0][:], in_=psG_[:], func=AF.Sigmoid), kG_, [("scrb", 0)])
                    psY_, kY_ = ps_pair()

                    def f(e, psY_=psY_, vD=vD, jj=jj):
                        last = None
                        for half in range(2):
                            for c in range(8):
                                last = e.matmul(psY_[:, half * 512:(half + 1) * 512], lhsT=vD[:, c, jj * 128:(jj + 1) * 128],
                                                rhs=zc[:, c, half * 512:(half + 1) * 512], start=(c == 0), stop=(c == 7))
                        return last
                    pe(f, [rkD] + zck, kY_)
                    dve(lambda e, psY_=psY_: e.tensor_tensor(out=scr[0][:], in0=scrb[0][:], in1=psY_[:], op=ALU.mult), kY_ + [("scrb", 0)], [("scr", 0)])
                    dve(lambda e, j=j: e.tensor_tensor(out=mi[:, j, :], in0=mi[:, j, :], in1=scr[0][:], op=ALU.add), [("scr", 0), ("mi", j)], [("mi", j)])
            mik = [("mi", j) for j in range(NT)]
            for g in range(2):
                rD, rkD = wget()
                vD = rD[:, 0:4096].rearrange("p (c n) -> p c n", c=8)
                for jj in range(4):
                    j = g * 4 + jj
                    psY_, kY_ = ps_pair()

                    def f(e, psY_=psY_, vD=vD, jj=jj):
                        last = None
                        for half in range(2):
                            for c in range(8):
                                last = e.matmul(psY_[:, half * 512:(half + 1) * 512], lhsT=vD[:, c, jj * 128:(jj + 1) * 128],
                                                rhs=mi[:, c, half * 512:(half + 1) * 512], start=(c == 0), stop=(c == 7))
                        return last
                    pe(f, [rkD] + mik, kY_)
                    gc_ = 24 + 8 + j
                    dve(lambda e, j=j, psY_=psY_, gc_=gc_: e.scalar_tensor_tensor(out=xT[:, j, :], in0=psY_[:], scalar=modx[:, gc_:gc_ + 1], in1=xT[:, j, :],
                                                                                 op0=ALU.mult, op1=ALU.add),
                        kY_ + xkeys(j) + [("modg", 1)], xkeys(j))
            return dnk + mik + zck + ogk

        ada_group(0, 6)
        mod_derive(0, 72, 0.5)
        ffn(0, 0, ada_ride=True)
        mod_derive(1, 80, 1.0)
        mod_derive(2, 88, 0.5)
        if STAGE >= 2:
            mkeys = mixer()
            dve(lambda e: e.memset(mx[:, 40:41], 0.0), [], mkeys + [("hT", ft) for ft in range(NFT)])
        ffn(1, 2)

        norm(3)
        yT = arena[:, 0:16384].bitcast(F32).rearrange("p (a n) -> p a n", a=8)
        for j in range(NT):
            dve(lambda e, j=j: e.scalar_tensor_tensor(out=yT[:, j, :], in0=xT[:, j, :], scalar=PT[:, 96 + j:97 + j], in1=rstd[:],
                                                      op0=ALU.mult, op1=ALU.mult),
                xkeys(j) + ["rstd", "PT"], [("yT", j)] + [("hT", ft) for ft in range(NFT)])
        ykeys = [("yT", j) for j in range(NT)]
        for tt in range(8):
            st = scr[tt % 2]
            for dh in range(2):
                psY, kY = ps_pair()

                def f(e, tt=tt, dh=dh, psY=psY):
                    last = None
                    for q in range(4):
                        dj = dh * 4 + q
                        last = e.transpose(psY[:, q * 128:(q + 1) * 128], yT[:, dj, tt * 128:(tt + 1) * 128], ident32)
                    return last
                pe(f, ykeys + ["cf"], kY[:1])
                if dh == 0:
                    dve(lambda e, st=st, psY=psY: e.tensor_copy(out=st[:, 0:512], in_=psY[:, 0:512]), kY[:1], [("scr", tt % 2, 0)] if False else [("scr", tt % 2)])
                else:
                    act(lambda e, st=st, psY=psY: e.activation(out=st[:, 512:1024], in_=psY[:, 0:512], func=AF.Identity), kY[:1], [("scr2", tt % 2)])
            hw_dma(y_out[tt * 128:(tt + 1) * 128, :], st[:], [("scr", tt % 2), ("scr2", tt % 2)], [("yout", tt)], ("st_y", tt))
            S.readers.setdefault(("scr2", tt % 2), [])

        outkeys += [("yout", tt) for tt in range(8)]
        S.op("sp", lambda e: None, outkeys, [])
        S.op("pool", lambda e: None, [("ring", s) for s in range(RING)] + ["cb"] + ([("s0", 0), ("s0", 1)] if STAGE >= 2 else []), [])

        block = es.enter_context(nc.Block())

        @block.tensor
        def _(e):
            S.emit("pe", e)

        @block.scalar
        def _(e):
            S.emit("act", e)

        @block.vector
        def _(e):
            S.emit("dve", e)

        @block.gpsimd
        def _(e):
            S.emit("pool", e)

        @block.sync
        def _(e):
            S.emit("sp", e)
    _DBG["map"] = dbg_map
    return nc


def _consts():
    cf = np.zeros((128, 192), np.float32)
    cf[:, 0:64] = np.arange(64, dtype=np.float32)[None, :]
    cf[:, 64:192] = np.eye(128, dtype=np.float32)
    cb = np.zeros((128, 1152), np.float32)
    cb[:, 0:128] = np.eye(128)
    cb[:, 128:256] = 1.0
    blk = np.zeros((128, 128), np.float32)
    blk[0:64, 0:64] = 1
    blk[64:128, 64:128] = 1
    i = np.arange(128)
    lower = (i[:, None] > i[None, :]).astype(np.float32)
    cb[:, 256:384] = lower * blk
    cb[:, 384:512] = lower.T * blk
    cb[:, 512:640] = (lower.T + np.eye(128)) * blk
    cb[:, 640:768] = (lower + np.eye(128)) * blk
    cb[:, 768:896] = (i[:, None] // 16 == i[None, :] // 16)
    for o_, b_ in ((896, 16), (1024, 32)):
        cb[:, o_:o_ + 128] = -1.0 * ((i[:, None] // (2 * b_) == i[None, :] // (2 * b_)) & (i[:, None] // b_ != i[None, :] // b_))
    return cf, cb


_NC_CACHE = {}


def kernel(x_prompt, x_sample, state_dn_fwd, state_dn_bwd, c, c_ctx, ada_w, ada_b,
           norm_ffn1, ffn1_w_gate, ffn1_w_up, ffn1_w_down, norm_mix, w_in, conv_w,
           conv_out_w, dn_conv_w, dn_a_log, dn_dt_bias, dn_norm_w, dn_out_w, w_o,
           norm_ffn2, ffn2_w_gate, ffn2_w_up, ffn2_w_down, norm_f):
    f32 = np.float32
    A = lambda a: np.ascontiguousarray(np.asarray(a, dtype=f32))
    x_prompt, x_sample = A(x_prompt), A(x_sample)
    cf, cb = _consts()
    shared = {
        "cf": cf, "cb": cb,
        "ada_w": A(ada_w)[0], "ffn1_w_gate": A(ffn1_w_gate)[0], "ffn1_w_up": A(ffn1_w_up)[0], "ffn1_w_down": A(ffn1_w_down)[0],
        "ffn2_w_gate": A(ffn2_w_gate)[0], "ffn2_w_up": A(ffn2_w_up)[0], "ffn2_w_down": A(ffn2_w_down)[0],
        "w_in": A(w_in)[0], "conv_out_w": A(conv_out_w)[0], "dn_out_w": A(dn_out_w)[0], "w_o": A(w_o)[0],
    }
    in_maps = []
    for core in range(8):
        is_sample = core < 2
        if is_sample:
            xc = x_sample[core]
            cond = A(c)[core]
            s0f = A(state_dn_fwd)[core, 0]
            s0b = A(state_dn_bwd)[core, 0]
        else:
            g = (core - 2) % 4
            xc = x_prompt[4 * g:4 * g + 4].reshape(T, D)
            cond = A(c_ctx)
            s0f = np.zeros((H, 128, 128), f32)
            s0b = np.zeros((H, 128, 128), f32)
        pab = np.zeros((256, 128), f32)
        pab[0:72] = A(ada_b)[0].reshape(72, 128)
        pab[72:80] = A(norm_ffn1)[0].reshape(8, 128)
        pab[80:88] = A(norm_mix)[0].reshape(8, 128)
        pab[88:96] = A(norm_ffn2)[0].reshape(8, 128)
        pab[96:104] = A(norm_f).reshape(8, 128)
        pab[104:112] = cond.reshape(8, 128)
        pab[112:136] = A(conv_w)[0].reshape(24, 128)
        pab[136:208] = A(dn_conv_w)[0].reshape(72, 128)
        cl = np.zeros((128, 16), f32)
        cl[:, 0] = A(dn_norm_w)[0]
        cl[0:16, 1] = A(dn_a_log)[0].reshape(16)
        cl[0:16, 2] = A(dn_dt_bias)[0].reshape(16)
        cl[:, 3] = 1.0 if is_sample else 0.0
        cl[:, 4] = 1.0 if is_sample else 0.0
        cl[:, 5] = 1.0 if is_sample else 0.0
        cl[8:16, 6] = 1.0
        cl[:, 7] = np.arange(128, dtype=f32)
        cl[:, 8] = np.arange(128, dtype=f32) + 128
        m = dict(shared)
        m.update({"x": np.ascontiguousarray(xc), "s0f": s0f, "s0b": s0b, "pab": pab, "cl": cl})
        in_maps.append(m)
    if "nc" not in _NC_CACHE:
        _NC_CACHE["nc"] = build_nc()
    nc = _NC_CACHE["nc"]
    res = run_bass_kernel_spmd(nc, in_maps, core_ids=list(range(8)))
    R = res.results
    if DBG:
        _DBG["res"] = R
    y_sample = np.stack([R[0]["y"], R[1]["y"]], axis=0).astype(f32)
    y_prompt = np.concatenate([R[2 + g]["y"].reshape(4, 256, D) for g in range(4)], axis=0).astype(f32)
    nsf = np.concatenate([R[2 + g]["sf"] for g in range(4)], axis=0)[:, None].astype(f32)
    nsb = np.concatenate([R[2 + g]["sb"] for g in range(4)], axis=0)[:, None].astype(f32)
    return (y_prompt, y_sample, nsf, nsb)
```
